# Optimizing a Trainium2 kernel written in Bass

```python
import jax
import jax.numpy as jnp
from jax import lax
import numpy as np

D_MODEL = 1024
BATCH = 4
SEQ = 8192
DEPTH = 4

GRID_W = 64
CTX_LEN = 256
N_MIXERS = 3
EPS = 1e-6

GLA_HEADS = 4
GLA_DK = D_MODEL // 2
GLA_DV = D_MODEL
GLA_HK = GLA_DK // GLA_HEADS
GLA_HV = GLA_DV // GLA_HEADS
GLA_GATE_RANK = 16
GLA_TAU = 16.0
GLA_CHUNK = 64

ATT_HEAD_DIM = 64
ATT_Q_HEADS = D_MODEL // ATT_HEAD_DIM
ATT_KV_HEADS = 4
ATT_GROUP = ATT_Q_HEADS // ATT_KV_HEADS
ATT_Q_WIDTH = ATT_Q_HEADS * ATT_HEAD_DIM
ATT_KV_WIDTH = ATT_KV_HEADS * ATT_HEAD_DIM
ATT_SCALE = ATT_HEAD_DIM ** -0.5
WINDOW = 128
ATT_BLOCK = 128
ROPE_PAIRS = ATT_HEAD_DIM // 4
ROPE_BASE = 10000.0

RNN_WIDTH = 1280
RNN_HEADS = 10
RNN_HD = RNN_WIDTH // RNN_HEADS
CONV_W = 4
CONV_LEFT = 2
LRU_C = 8.0

kernel_name = 'hybrid_gla_swa_rglru_prefix_dit'


def rmsnorm(x, g):
    xf = x.astype(jnp.float32)
    y = xf * lax.rsqrt(jnp.mean(xf * xf, axis=-1, keepdims=True) + EPS)
    return (y * g.astype(jnp.float32)).astype(x.dtype)


def split_heads(a, n):
    b, t, w = a.shape
    return a.reshape(b, t, n, w // n).transpose(0, 2, 1, 3)


def merge_heads(a):
    b, n, t, d = a.shape
    return a.transpose(0, 2, 1, 3).reshape(b, t, n * d)


def flip_t(a):
    return jnp.flip(a, axis=2)


def gla_chunked(q, k, v, log_a, s0):
    b, h, t, _ = q.shape
    n = t // GLA_CHUNK

    def to_chunks(a):
        return jnp.moveaxis(a.reshape(b, h, n, GLA_CHUNK, a.shape[-1]), 2, 0)

    cum = jnp.cumsum(to_chunks(log_a), axis=-2)
    lower = jnp.tril(jnp.ones((GLA_CHUNK, GLA_CHUNK), dtype=bool))[:, :, None]

    def step(s, inp):
        q_n, k_n, v_n, b_n = inp
        rel = b_n[:, :, :, None, :] - b_n[:, :, None, :, :]
        decay = jnp.exp(jnp.where(lower, rel, -jnp.inf))
        scores = jnp.einsum('bhtd,bhsd,bhtsd->bhts', q_n, k_n, decay)
        o = scores @ v_n + (q_n * jnp.exp(b_n)) @ s
        b_last = b_n[:, :, -1:, :]
        s = jnp.exp(b_last)[:, :, 0, :, None] * s + jnp.swapaxes(k_n * jnp.exp(b_last - b_n), -1, -2) @ v_n
        return s, o

    s_fin, o = lax.scan(step, s0, (to_chunks(q), to_chunks(k), to_chunks(v), cum))
    return jnp.moveaxis(o, 0, 2).reshape(b, h, t, v.shape[-1]), s_fin


def gla_final_state(k, v, log_a):
    cum = jnp.cumsum(log_a, axis=2)
    return jnp.swapaxes(k * jnp.exp(cum[:, :, -1:, :] - cum), -1, -2) @ v


def gla_bidir(q, k, v, la_f, la_b, s0_f, s0_b):
    o_f, s_f = gla_chunked(q, k, v, la_f, s0_f)
    o_b, s_b = gla_chunked(flip_t(q), flip_t(k), flip_t(v), flip_t(la_b), s0_b)
    return o_f + flip_t(o_b), s_f, s_b


def mixer_gla(h_lat, h_ctx, w_in, w_g1, w_g2, b_g, g_head, w_out, ctx_out):
    def project(h):
        q, k, v, z = jnp.split(h @ w_in, [GLA_DK, 2 * GLA_DK, 2 * GLA_DK + GLA_DV], axis=-1)
        la_f, la_b = [split_heads(jax.nn.log_sigmoid(((h @ w_g1[d]) @ w_g2[d] + b_g[d]).astype(jnp.float32)) / GLA_TAU, GLA_HEADS)
                      for d in range(2)]
        q = split_heads(q, GLA_HEADS).astype(jnp.float32) * GLA_HK ** -0.5
        k = split_heads(k, GLA_HEADS).astype(jnp.float32)
        v = split_heads(v, GLA_HEADS).astype(jnp.float32)
        return q, k, v, z, la_f, la_b

    def finish(o, z):
        o = merge_heads(rmsnorm(o, g_head[:, None, :]))
        return (o.astype(z.dtype) * jax.nn.silu(z)) @ w_out

    q_c, k_c, v_c, z_c, la_cf, la_cb = project(h_ctx)
    q_l, k_l, v_l, z_l, la_lf, la_lb = project(h_lat)
    if ctx_out:
        zero = jnp.zeros((h_ctx.shape[0], GLA_HEADS, GLA_HK, GLA_HV), jnp.float32)
        o_c, s_f, s_b = gla_bidir(q_c, k_c, v_c, la_cf, la_cb, zero, zero)
        y_ctx = finish(o_c, z_c)
    else:
        s_f = gla_final_state(k_c, v_c, la_cf)
        s_b = gla_final_state(flip_t(k_c), flip_t(v_c), flip_t(la_cb))
        y_ctx = None
    o_l, _, _ = gla_bidir(q_l, k_l, v_l, la_lf, la_lb, s_f, s_b)
    return finish(o_l, z_l), y_ctx


def rope_1d(x, cos, sin):
    x1, x2 = jnp.split(x, 2, axis=-1)
    return jnp.concatenate([x1 * cos - x2 * sin, x2 * cos + x1 * sin], axis=-1)


def rope_axial(x, rope):
    cos_r, sin_r, cos_c, sin_c = rope
    half = ATT_HEAD_DIM // 2
    return jnp.concatenate([rope_1d(x[..., :half], cos_r, sin_r), rope_1d(x[..., half:], cos_c, sin_c)], axis=-1)


def attn_project(h, w_in):
    b, t, _ = h.shape
    q, k, v, z = jnp.split(h @ w_in, [ATT_Q_WIDTH, ATT_Q_WIDTH + ATT_KV_WIDTH, ATT_Q_WIDTH + 2 * ATT_KV_WIDTH], axis=-1)
    q = q.reshape(b, t, ATT_KV_HEADS, ATT_GROUP, ATT_HEAD_DIM).transpose(0, 2, 3, 1, 4)
    k = k.reshape(b, t, ATT_KV_HEADS, ATT_HEAD_DIM).transpose(0, 2, 1, 3)
    v = v.reshape(b, t, ATT_KV_HEADS, ATT_HEAD_DIM).transpose(0, 2, 1, 3)
    return q, k, v, z


def sink_column(sink, lead_shape):
    return jnp.broadcast_to(sink.astype(jnp.float32)[None, :, :, None, None], lead_shape + (1,))


def context_attention(q, k, v, sink):
    s = jnp.einsum('bkgqd,bksd->bkgqs', q, k).astype(jnp.float32) * ATT_SCALE
    p = jax.nn.softmax(jnp.concatenate([s, sink_column(sink, s.shape[:-1])], axis=-1), axis=-1)[..., :-1]
    return jnp.einsum('bkgqs,bksd->bkgqd', p.astype(v.dtype), v)


def banded_attention(q, k, v, k_ctx, v_ctx, sink):
    b, hk, g, t, dh = q.shape
    n = t // ATT_BLOCK
    span = 3 * ATT_BLOCK
    pad = ((0, 0), (0, 0), (ATT_BLOCK, ATT_BLOCK), (0, 0))
    kp, vp = jnp.pad(k, pad), jnp.pad(v, pad)
    q_blocks = jnp.moveaxis(q.reshape(b, hk, g, n, ATT_BLOCK, dh), 3, 0)
    offs = jnp.arange(span) - ATT_BLOCK
    in_window = jnp.abs(offs[None, :] - jnp.arange(ATT_BLOCK)[:, None]) <= WINDOW

    def block(args):
        q_n, idx = args
        start = idx * ATT_BLOCK
        k_n = lax.dynamic_slice_in_dim(kp, start, span, axis=2)
        v_n = lax.dynamic_slice_in_dim(vp, start, span, axis=2)
        kpos = start + offs
        valid = in_window & ((kpos >= 0) & (kpos < t))[None, :]
        s_loc = jnp.einsum('bkgqd,bksd->bkgqs', q_n, k_n).astype(jnp.float32) * ATT_SCALE
        s_loc = jnp.where(valid, s_loc, -jnp.inf)
        s_ctx = jnp.einsum('bkgqd,bksd->bkgqs', q_n, k_ctx).astype(jnp.float32) * ATT_SCALE
        logits = jnp.concatenate([s_loc, s_ctx, sink_column(sink, s_loc.shape[:-1])], axis=-1)
        p = jax.nn.softmax(logits, axis=-1).astype(v.dtype)
        return (jnp.einsum('bkgqs,bksd->bkgqd', p[..., :span], v_n)
                + jnp.einsum('bkgqs,bksd->bkgqd', p[..., span:-1], v_ctx))

    o = lax.map(block, (q_blocks, jnp.arange(n)))
    return jnp.moveaxis(o, 0, 3).reshape(b, hk, g, t, dh)


def mixer_swa(h_lat, h_ctx, w_in, sink, w_out, rope, ctx_out):
    sink = sink.reshape(ATT_KV_HEADS, ATT_GROUP)
    q_c, k_c, v_c, z_c = attn_project(h_ctx, w_in)
    q_l, k_l, v_l, z_l = attn_project(h_lat, w_in)
    q_l, k_l = rope_axial(q_l, rope), rope_axial(k_l, rope)

    def finish(o, z):
        b, hk, g, t, d = o.shape
        o = o.transpose(0, 3, 1, 2, 4).reshape(b, t, hk * g * d)
        return (o * jax.nn.silu(z)) @ w_out

    y_lat = finish(banded_attention(q_l, k_l, v_l, k_c, v_c, sink), z_l)
    y_ctx = finish(context_attention(q_c, k_c, v_c, sink), z_c) if ctx_out else None
    return y_lat, y_ctx


def depthwise_conv(u, w, bias):
    t = u.shape[1]
    up = jnp.pad(u, ((0, 0), (CONV_LEFT, CONV_W - 1 - CONV_LEFT), (0, 0)))
    out = up[:, 0:t] * w[0]
    for j in range(1, CONV_W):
        out = out + up[:, j:j + t] * w[j]
    return out + bias


def block_diag(x, w, bias):
    xh = x.reshape(x.shape[:-1] + (RNN_HEADS, RNN_HD))
    return jnp.einsum('bthi,hij->bthj', xh, w).reshape(x.shape) + bias


def rglru_gates(u, w_a, b_a, w_x, b_x, lam):
    r = jax.nn.sigmoid(block_diag(u, w_a, b_a).astype(jnp.float32))
    i = jax.nn.sigmoid(block_diag(u, w_x, b_x).astype(jnp.float32))
    log_a = -LRU_C * r * jax.nn.softplus(-lam.astype(jnp.float32))
    x = jnp.sqrt(-jnp.expm1(2.0 * log_a)) * i * u.astype(jnp.float32)
    return jnp.exp(log_a), x


def scan_combine(left, right):
    a_l, h_l = left
    a_r, h_r = right
    return a_l * a_r, a_r * h_l + h_r


def linear_scan(a, x, h0, reverse):
    edge = -1 if reverse else 0
    x = x.at[:, edge].add(a[:, edge] * h0)
    return lax.associative_scan(scan_combine, (a, x), reverse=reverse, axis=1)[1]


def mixer_rglru(h_lat, h_ctx, w_in, conv_w, conv_b, w_ra, b_ra, w_ri, b_ri, lam, w_out, ctx_out):
    def branch(h):
        u, z = jnp.split(h @ w_in, 2, axis=-1)
        return depthwise_conv(u, conv_w, conv_b), z

    u_c, z_c = branch(h_ctx)
    u_l, z_l = branch(h_lat)
    h0 = jnp.zeros((h_ctx.shape[0], RNN_WIDTH), jnp.float32)
    hs_c, hs_l = [], []
    for d, reverse in ((0, False), (1, True)):
        a_c, x_c = rglru_gates(u_c, w_ra[d], b_ra[d], w_ri[d], b_ri[d], lam[d])
        h_c = linear_scan(a_c, x_c, h0, reverse)
        a_l, x_l = rglru_gates(u_l, w_ra[d], b_ra[d], w_ri[d], b_ri[d], lam[d])
        hs_l.append(linear_scan(a_l, x_l, h_c[:, 0 if reverse else -1], reverse))
        hs_c.append(h_c)

    def finish(hs, z):
        return ((hs[0] + hs[1]).astype(z.dtype) * jax.nn.silu(z)) @ w_out

    return finish(hs_l, z_l), (finish(hs_c, z_c) if ctx_out else None)


def setup_inputs(seed: int = 0) -> dict:
    key = jax.random.key(seed)
    keys = iter(jax.random.split(key, 32))

    def normal(shape, scale):
        return jax.random.normal(next(keys), shape, jnp.float32) * scale

    d = D_MODEL
    n_a = len(range(0, DEPTH, N_MIXERS))
    n_b = len(range(1, DEPTH, N_MIXERS))
    n_c = len(range(2, DEPTH, N_MIXERS))
    a_pow_c = jax.random.uniform(next(keys), (n_c, 2, RNN_WIDTH), jnp.float32, 0.9, 0.999)
    s = a_pow_c ** (1.0 / LRU_C)
    c_lam = jnp.log(s) - jnp.log1p(-s)
    return {
        'x': normal((BATCH, SEQ, d), 1.0),
        'c': normal((BATCH, d), 1.0),
        'ctx': normal((BATCH, CTX_LEN, d), 1.0),
        'c_ctx': normal((d,), 1.0),
        'w_mod': normal((DEPTH, d, 3 * d), 0.5 * d ** -0.5),
        'b_mod': normal((DEPTH, 3 * d), 0.02),
        'g_pre': 1.0 + normal((DEPTH, d), 0.05),
        'g_post': 1.0 + normal((DEPTH, d), 0.05),
        'a_w_in': normal((n_a, d, 2 * GLA_DK + 2 * GLA_DV), d ** -0.5),
        'a_w_g1': normal((n_a, 2, d, GLA_GATE_RANK), d ** -0.5),
        'a_w_g2': normal((n_a, 2, GLA_GATE_RANK, GLA_DK), GLA_GATE_RANK ** -0.5),
        'a_b_g': normal((n_a, 2, GLA_DK), 0.1),
        'a_g_head': 1.0 + normal((n_a, GLA_HEADS, GLA_HV), 0.05),
        'a_w_out': normal((n_a, GLA_DV, d), GLA_DV ** -0.5),
        'b_w_in': normal((n_b, d, 2 * ATT_Q_WIDTH + 2 * ATT_KV_WIDTH), d ** -0.5),
        'b_sink': normal((n_b, ATT_Q_HEADS), 0.5),
        'b_w_out': normal((n_b, ATT_Q_WIDTH, d), ATT_Q_WIDTH ** -0.5),
        'c_w_in': normal((n_c, d, 2 * RNN_WIDTH), d ** -0.5),
        'c_conv_w': normal((n_c, CONV_W, RNN_WIDTH), CONV_W ** -0.5),
        'c_conv_b': normal((n_c, RNN_WIDTH), 0.02),
        'c_w_ra': normal((n_c, 2, RNN_HEADS, RNN_HD, RNN_HD), RNN_HD ** -0.5),
        'c_b_ra': normal((n_c, 2, RNN_WIDTH), 0.02),
        'c_w_ri': normal((n_c, 2, RNN_HEADS, RNN_HD, RNN_HD), RNN_HD ** -0.5),
        'c_b_ri': normal((n_c, 2, RNN_WIDTH), 0.02),
        'c_lam': c_lam,
        'c_w_out': normal((n_c, RNN_WIDTH, d), RNN_WIDTH ** -0.5),
    }


def reference(x, c, ctx, c_ctx, w_mod, b_mod, g_pre, g_post,
              a_w_in, a_w_g1, a_w_g2, a_b_g, a_g_head, a_w_out,
              b_w_in, b_sink, b_w_out,
              c_w_in, c_conv_w, c_conv_b, c_w_ra, c_b_ra, c_w_ri, c_b_ri, c_lam, c_w_out):
    t = x.shape[1]
    rows = t // GRID_W
    row = jnp.repeat(jnp.arange(rows, dtype=jnp.float32), GRID_W)
    col = jnp.tile(jnp.arange(GRID_W, dtype=jnp.float32), rows)
    freqs = ROPE_BASE ** (-jnp.arange(ROPE_PAIRS, dtype=jnp.float32) / ROPE_PAIRS)
    ang_r = row[:, None] * freqs
    ang_c = col[:, None] * freqs
    rope = (jnp.cos(ang_r).astype(x.dtype), jnp.sin(ang_r).astype(x.dtype),
            jnp.cos(ang_c).astype(x.dtype), jnp.sin(ang_c).astype(x.dtype))

    s_lat = jax.nn.silu(c)
    s_ctx = jax.nn.silu(c_ctx)
    xc = ctx
    for i in range(DEPTH):
        kind, j = i % N_MIXERS, i // N_MIXERS
        ctx_out = i < DEPTH - 1
        shift, scale, gate = jnp.split(s_lat @ w_mod[i] + b_mod[i], 3, axis=-1)
        shift_c, scale_c, gate_c = jnp.split(s_ctx @ w_mod[i] + b_mod[i], 3, axis=-1)
        h = rmsnorm(x, g_pre[i]) * (1.0 + scale[:, None, :]) + shift[:, None, :]
        hc = rmsnorm(xc, g_pre[i]) * (1.0 + scale_c) + shift_c
        if kind == 0:
            y, yc = mixer_gla(h, hc, a_w_in[j], a_w_g1[j], a_w_g2[j], a_b_g[j], a_g_head[j], a_w_out[j], ctx_out)
        elif kind == 1:
            y, yc = mixer_swa(h, hc, b_w_in[j], b_sink[j], b_w_out[j], rope, ctx_out)
        else:
            y, yc = mixer_rglru(h, hc, c_w_in[j], c_conv_w[j], c_conv_b[j], c_w_ra[j], c_b_ra[j],
                                c_w_ri[j], c_b_ri[j], c_lam[j], c_w_out[j], ctx_out)
        x = x + gate[:, None, :] * rmsnorm(y, g_post[i])
        if ctx_out:
            xc = xc + gate_c * rmsnorm(yc, g_post[i])
    return x
```

```python
import contextlib
import numpy as np
import concourse.bass as bass
import concourse.mybir as mybir
from concourse.bass_utils import run_bass_kernel_spmd

F32 = mybir.dt.float32
BF16 = mybir.dt.bfloat16
AF = mybir.ActivationFunctionType
ALU = mybir.AluOpType
AX = mybir.AxisListType


SEM_WRAP = 2000


class Buf:
    __slots__ = ("name", "t", "last_w", "readers")

    def __init__(self, name, t=None):
        self.name = name
        self.t = t
        self.last_w = None
        self.readers = []


class Sched:
    ENG = ("pe", "act", "dve", "pool", "sp")

    def __init__(self, nc, es):
        self.nc = nc
        self.es = es
        self.ops = []
        self.dma_keys = {}

    def sbuf(self, name, shape, dtype, arena=False):
        if arena:
            return self.carve(name, list(shape), dtype)
        t = self.es.enter_context(self.nc.sbuf_tensor("s_" + name, list(shape), dtype))
        return Buf(name, t)

    def psum(self, name, shape, dtype):
        t = self.es.enter_context(self.nc.psum_tensor("p_" + name, list(shape), dtype))
        return Buf(name, t)

    def dram_buf(self, name):
        return Buf(name, None)

    def set_arena(self, arena_t, nbytes):
        self.arena_t = arena_t
        self.arena_bytes = nbytes
        self.apos = 0

    def arena_reset(self):
        self.barrier()
        self.apos = 0

    def carve(self, name, shape, dtype, parts=128):
        n = 1
        for v in shape[1:]:
            n *= v
        esz = 4 if dtype == F32 else 2
        nb = (n * esz + 31) // 32 * 32
        assert self.apos + nb <= self.arena_bytes, (name, self.apos, nb, self.arena_bytes)
        v = self.arena_t[0:shape[0], self.apos // 2:(self.apos + n * esz) // 2]
        self.apos += nb
        if dtype == F32:
            v = v.bitcast(F32)
        if len(shape) == 3:
            v = v.rearrange("p (a b) -> p a b", a=shape[1])
        elif len(shape) == 4:
            v = v.rearrange("p (a b c) -> p a b c", a=shape[1], b=shape[2])
        return Buf(name, v)

    def barrier(self):
        last = {}
        dmas = set()
        for i, o in enumerate(self.ops):
            if o["fn"] is None:
                continue
            if o["dma"]:
                dmas.add(i)
            else:
                last[o["eng"]] = i
        deps = set(last.values()) | dmas
        for e in self.ENG:
            self.ops.append(dict(eng=e, fn=None, deps=set(deps), dma=False, ev=None, inc=16, bar=True))

    def view(self, name, t):
        return Buf(name, t)

    def op(self, eng, fn, reads=(), writes=(), acc=False, dma=False, key=None, inc=None):
        i = len(self.ops)
        deps = set()
        raw = set()
        for b in reads:
            if b.last_w is not None:
                deps.add(b.last_w)
                raw.add(b.last_w)
        for b in writes:
            if not acc:
                if b.last_w is not None:
                    deps.add(b.last_w)
                for r in b.readers:
                    deps.add(r)
        for b in reads:
            if dma:
                b.readers.append(i)
            else:
                b.readers = [r for r in b.readers if self.ops[r]["dma"] or self.ops[r]["eng"] != eng]
                b.readers.append(i)
        for b in writes:
            b.last_w = i
            if not acc:
                b.readers = []
        deps.discard(i)
        if dma:
            if key is None:
                key = (writes[0].name if writes else reads[0].name)
            key = key + "@" + eng
            cnt = self.dma_keys.get(key, 0) + 1
            self.dma_keys[key] = cnt
            ev = ("d:" + key, cnt * (inc if inc is not None else 16))
        else:
            ev = None
        self.ops.append(dict(eng=eng, fn=fn, deps=deps, raw=raw, dma=dma, ev=ev, inc=(inc if inc is not None else 16)))
        return i

    def dma(self, eng, out, in_, reads=(), writes=(), key=None):
        return self.op(eng, lambda e: e.dma_start(out=out, in_=in_, allow_slow_non_contiguous=True), reads=reads, writes=writes, dma=True, key=key)

    def _needed(self, o):
        ops = self.ops
        need = {}
        for dd in o["deps"]:
            od = ops[dd]
            if od["fn"] is None:
                continue
            if od["dma"]:
                key = od["ev"][0]
            else:
                if od["eng"] == o["eng"] and not o.get("bar") and not o["dma"]:
                    if od["eng"] == "pe":
                        continue
                key = "e:" + od["eng"]
            if dd > need.get(key, -1):
                need[key] = dd
        return need

    def finish(self):
        nc = self.nc
        ops = self.ops
        engobj = dict(pe=nc.tensor, act=nc.scalar, dve=nc.vector, pool=nc.gpsimd, sp=nc.sync)
        final_deps = set(i for i, o in enumerate(ops) if o["dma"] and o["fn"] is not None)
        ops.append(dict(eng="sp", fn=None, deps=final_deps, raw=set(), dma=False, ev=None, inc=16, bar=True))
        wm = {e: {} for e in self.ENG}
        waits = []
        marked = set()
        for i, o in enumerate(ops):
            w = []
            for key, dd in self._needed(o).items():
                if wm[o["eng"]].get(key, -1) >= dd:
                    continue
                wm[o["eng"]][key] = dd
                w.append(dd)
                marked.add(dd)
            waits.append(w)
        cnt = {e: 0 for e in self.ENG}
        for i, o in enumerate(ops):
            if o["fn"] is not None and not o["dma"] and i in marked:
                cnt[o["eng"]] += 1
                k_ = (cnt[o["eng"]] - 1) // SEM_WRAP
                o["ev"] = ("e:" + o["eng"] + str(k_), (cnt[o["eng"]] - 1) % SEM_WRAP + 1)
        sems = {}

        def sem(name):
            if name not in sems:
                sems[name] = self.es.enter_context(nc.semaphore(name.replace(":", "_").replace("@", "_")))
            return sems[name]

        n_inc = 0
        for i, o in enumerate(ops):
            for dd in waits[i]:
                s_, v_ = ops[dd]["ev"]
                engobj[o["eng"]].wait_ge(sem(s_), v_)
            if o["fn"] is None:
                continue
            ins = o["fn"](engobj[o["eng"]])
            if o["dma"]:
                ins.then_inc(sem(o["ev"][0]), o["inc"])
                n_inc += 1
            elif o["ev"] is not None:
                ins.then_inc(sem(o["ev"][0]), 1)
                n_inc += 1
        self.n_ops = len(ops)
        self.n_sems = len(sems)
        self.n_inc = n_inc


EPS = 1e-6
T_CORE = 4096
NT = 32
NCT = 2
ARENA_BYTES = 88 * 1024
KINDS = [0, 1, 2, 0]


class Rot:
    def __init__(self, S, name, shape, dtype, n, psum=False, arena=False):
        if psum:
            self.bufs = [S.psum(f"{name}{i}", shape, dtype) for i in range(n)]
        else:
            self.bufs = [S.sbuf(f"{name}{i}", shape, dtype, arena=arena) for i in range(n)]
        self.i = 0

    def next(self):
        b = self.bufs[self.i % len(self.bufs)]
        self.i += 1
        return b


class Prog:
    def __init__(self, n_layers=4, probe=None, stage=99):
        self.n_layers = n_layers
        self.stage = stage
        self.nc = nc = bass.Bass("TRN2", target_bir_lowering=False)
        self.ext_in = {}
        self.probe = probe

    def din(self, name, shape, dt=F32):
        t = self.nc.dram_tensor(name, list(shape), dt, kind="ExternalInput")
        self.ext_in[name] = t
        return t.ap()

    def build(self):
        nc = self.nc
        d = {}
        d["x"] = self.din("x", [T_CORE, 1024])
        d["ctx"] = self.din("ctx", [256, 1024])
        d["cT"] = self.din("cT", [128, 16])
        d["w_mod"] = self.din("w_mod", [4, 1024, 3072])
        d["b_mod"] = self.din("b_mod", [4, 3072])
        d["g_pre"] = self.din("g_pre", [4, 1024])
        d["g_post"] = self.din("g_post", [4, 1024])
        d["a_w_in"] = self.din("a_w_in", [2, 1024, 3072])
        d["a_w_g1"] = self.din("a_w_g1", [2, 2, 1024, 16])
        d["a_w_g2"] = self.din("a_w_g2", [2, 2, 16, 512])
        d["a_b_g"] = self.din("a_b_g", [2, 2, 512])
        d["a_g_head"] = self.din("a_g_head", [2, 4, 256])
        d["a_w_out"] = self.din("a_w_out", [2, 1024, 1024])
        d["b_w_in"] = self.din("b_w_in", [1, 1024, 2560])
        d["b_sink"] = self.din("b_sink", [1, 16])
        d["b_w_out"] = self.din("b_w_out", [1, 1024, 1024])
        d["c_w_in"] = self.din("c_w_in", [1, 1024, 2560])
        d["c_conv_w"] = self.din("c_conv_w", [1, 4, 1280])
        d["c_conv_b"] = self.din("c_conv_b", [1, 1280])
        d["c_w_ra"] = self.din("c_w_ra", [1, 2, 10, 128, 128])
        d["c_b_ra"] = self.din("c_b_ra", [1, 2, 1280])
        d["c_w_ri"] = self.din("c_w_ri", [1, 2, 10, 128, 128])
        d["c_b_ri"] = self.din("c_b_ri", [1, 2, 1280])
        d["c_lam"] = self.din("c_lam", [1, 2, 1280])
        d["c_w_out"] = self.din("c_w_out", [1, 1280, 1024])
        d["c_vecs"] = self.din("c_vecs", [128, 11, 10])
        d["ident"] = self.din("ident", [128, 128], BF16)
        d["lmats"] = self.din("lmats", [128, 4, 128])
        d["masks"] = self.din("masks", [128, 2, 4, 128], BF16)
        d["flags"] = self.din("flags", [128, 2])
        d["rope"] = self.din("rope", [T_CORE, 2, 64])
        self.d = d
        self.out = nc.dram_tensor("out", [T_CORE, 1024], F32, kind="ExternalOutput").ap()
        self.xa = nc.dram_tensor("xa", [T_CORE, 1024], F32).ap()
        self.xb = nc.dram_tensor("xb", [T_CORE, 1024], F32).ap()
        self.xca = nc.dram_tensor("xca", [256, 1024], F32).ap()
        self.xcb = nc.dram_tensor("xcb", [256, 1024], F32).ap()
        self.modrows = nc.dram_tensor("modrows", [4, 2, 3, 1024], F32).ap()
        self.SfD = nc.dram_tensor("SfD", [NT + NCT, 128, 1024], BF16).ap()
        self.SbD = nc.dram_tensor("SbD", [NT + NCT, 128, 1024], BF16).ap()
        self.DbD = nc.dram_tensor("DbD", [NT + NCT, 128, 1024], F32).ap()
        self.sctxD = nc.dram_tensor("sctxD", [2, 128, 1024], F32).ap()
        self.bounce = nc.dram_tensor("bounce", [256, 1024], F32)
        self.gath = nc.dram_tensor("gath", [512, 1024], F32)

        with contextlib.ExitStack() as es:
            self.S = S = Sched(nc, es)
            self.alloc()
            self.prologue()
            xin, xcin = d["x"], d["ctx"]
            xin_b = [S.dram_buf(f"xin{i}") for i in range(NT)]
            xcin_b = [S.dram_buf(f"xcin{i}") for i in range(NCT)]
            for l in range(self.n_layers):
                last = (l == self.n_layers - 1)
                xout = self.out if last else (self.xa if l % 2 == 0 else self.xb)
                xcout = self.xca if l % 2 == 0 else self.xcb
                xout_b = [S.dram_buf(f"xo{l}_{i}") for i in range(NT)]
                xcout_b = [S.dram_buf(f"xco{l}_{i}") for i in range(NCT)]
                ctx_out = not last
                kind = KINDS[l]
                S.arena_reset()
                self.cur_l = l
                if kind == 0:
                    self.layer_gla(l, l // 3, xin, xin_b, xcin, xcin_b, xout, xout_b, xcout, xcout_b, ctx_out)
                elif kind == 1:
                    self.layer_swa(l, xin, xin_b, xcin, xcin_b, xout, xout_b, xcout, xcout_b, ctx_out)
                else:
                    self.layer_lru(l, xin, xin_b, xcin, xcin_b, xout, xout_b, xcout, xcout_b, ctx_out)
                xin, xin_b, xcin, xcin_b = xout, xout_b, xcout, xcout_b
            if self.stage < 6:
                S.dma("sp", self.out, d["x"], key="dbgcopy")
            S.finish()
        return nc

    def alloc(self):
        S = self.S
        self.wbuf = S.sbuf("wbuf", [128, 35840], BF16)
        self.arena = S.sbuf("arena", [128, ARENA_BYTES // 2], BF16)
        S.set_arena(self.arena.t, ARENA_BYTES)
        self.stg = Rot(S, "stg", [128, 1024], F32, 2)
        self.ident = S.sbuf("identb", [128, 128], BF16)
        self.lmats = S.sbuf("lmatsb", [128, 4, 128], F32)
        self.masks = S.sbuf("masksb", [128, 2, 4, 128], BF16)
        self.flags = S.sbuf("flagsb", [128, 2], F32)
        self.negcol = S.sbuf("negcol", [128, 1], F32)
        self.rows = S.sbuf("rows", [128, 3, 1024], F32)
        self.xt = Rot(S, "xt", [128, 1024], F32, 2)
        self.t32 = Rot(S, "t32", [128, 1024], F32, 1)
        self.hb = Rot(S, "hb", [128, 1024], BF16, 2)
        self.junkB = S.sbuf("junkB", [128, 1024], BF16)
        self.hT = Rot(S, "hT", [128, 8, 128], BF16, 2)
        self.sm = Rot(S, "sm", [128, 8], F32, 12)
        self.pb = Rot(S, "pb", [128, 512], F32, 6, psum=True)
        self.pt = Rot(S, "pt", [128, 512], F32, 2, psum=True)
        for nm, buf in (("ident", self.ident), ("lmats", self.lmats), ("masks", self.masks), ("flags", self.flags)):
            S.dma("sp", buf.t[:], self.d[nm], writes=[buf])
        S.op("pool", lambda e: e.memset(self.negcol.t[:], -1.0 / 16.0), writes=[self.negcol])
        self.bb, self.gb = S.dram_buf("bounce"), S.dram_buf("gath")
        z = self.t32.next()
        S.op("pool", lambda e: e.memset(z.t[:], 0.0), writes=[z])
        S.dma("sp", self.bounce.ap()[0:128, :], z.t[:], reads=[z], writes=[self.bb], key="bnc0")
        S.dma("sp", self.bounce.ap()[128:256, :], z.t[:], reads=[z], writes=[self.bb], key="bnc0")

    def prologue(self):
        S, d = self.S, self.d
        cT = S.carve("cT", [128, 16], F32)
        sT = S.carve("sT", [128, 16], F32)
        S.dma("sp", cT.t[:], d["cT"], writes=[cT])
        S.op("act", lambda e: e.activation(out=sT.t[:], in_=cT.t[:], func=AF.Silu), reads=[cT], writes=[sT])
        mrow = S.carve("mrow", [2, 3072], F32)
        brow = S.carve("brow", [2, 3072], F32)
        grow = S.carve("grow", [2, 2, 1024], F32)
        orow = S.carve("orow", [2, 3, 1024], F32)
        mdb = S.dram_buf("modrows")
        self.modrows_b = mdb
        for l in range(self.n_layers):
            S.dma("sp", brow.t[:], d["b_mod"][l].partition_broadcast(2), writes=[brow])
            S.dma("sp", grow.t[:, 0, :], d["g_pre"][l].partition_broadcast(2), writes=[grow])
            S.dma("sp", grow.t[:, 1, :], d["g_post"][l].partition_broadcast(2), writes=[grow])
            wv = d["w_mod"][l].rearrange("(k p) n -> p k n", p=128)
            for n in range(24):
                st = self.stg.next()
                stv = st.t[:].rearrange("p (k n) -> p k n", k=8)
                S.dma("sp" if n % 2 == 0 else "pool", stv, wv[:, :, n * 128:(n + 1) * 128], writes=[st])
                ps = self.pb.next()
                for kc in range(8):
                    S.op("pe", lambda e, kc=kc, ps=ps, stv=stv: e.matmul(ps.t[0:2, 0:128], lhsT=sT.t[:, 2 * kc:2 * kc + 2], rhs=stv[:, kc, :], start=(kc == 0), stop=(kc == 7)),
                         reads=[sT, st], writes=[ps], acc=(kc > 0))
                S.op("dve", lambda e, ps=ps, n=n: e.tensor_tensor(out=mrow.t[:, n * 128:(n + 1) * 128], in0=ps.t[0:2, 0:128], in1=brow.t[:, n * 128:(n + 1) * 128], op=ALU.add),
                     reads=[ps, brow], writes=[mrow])
            S.op("dve", lambda e: e.tensor_copy(out=orow.t[:, 0, :], in_=mrow.t[:, 0:1024]), reads=[mrow], writes=[orow])
            S.op("dve", lambda e: e.scalar_tensor_tensor(out=orow.t[:, 1, :], in0=mrow.t[:, 1024:2048], scalar=1.0, in1=grow.t[:, 0, :], op0=ALU.add, op1=ALU.mult),
                 reads=[mrow, grow], writes=[orow])
            S.op("dve", lambda e: e.tensor_tensor(out=orow.t[:, 2, :], in0=mrow.t[:, 2048:3072], in1=grow.t[:, 1, :], op=ALU.mult),
                 reads=[mrow, grow], writes=[orow])
            S.dma("sp", self.modrows[l], orow.t[:], reads=[orow], writes=[mdb], key="modrows_st")

    def load_rows(self, l, r):
        S = self.S
        S.dma("sp", self.rows.t[:], self.modrows[l, r].rearrange("a n -> (a n)").partition_broadcast(128).rearrange("p (a n) -> p a n", a=3),
              reads=[self.modrows_b], writes=[self.rows])
        return self.rows

    def load_cast(self, dst_ap, src_ap, shape, dst_buf, i=0):
        S = self.S
        st = self.stg.next()
        n = shape[1] * shape[2]
        stv = st.t[:, 0:n].rearrange("p (a b) -> p a b", a=shape[1])
        S.dma("sp" if i % 2 == 0 else "pool", stv, src_ap, writes=[st])
        eng = ("pool", "act", "dve")[i % 3]
        if eng == "act":
            S.op("act", lambda e: e.copy(out=dst_ap, in_=stv), reads=[st], writes=[dst_buf])
        else:
            S.op(eng, lambda e: e.tensor_copy(out=dst_ap, in_=stv), reads=[st], writes=[dst_buf])

    def load_w(self, off, src, ncols, nk=8):
        wv = self.wbuf.t[:, off:off + nk * ncols].rearrange("p (k n) -> p k n", k=nk)
        sv = src.rearrange("(k p) n -> p k n", p=128)
        step = max(1, 1024 // nk)
        step = min(step, ncols)
        i = 0
        for n0 in range(0, ncols, step):
            n1 = min(ncols, n0 + step)
            self.load_cast(wv[:, :, n0:n1], sv[:, :, n0:n1], [128, nk, n1 - n0], self.wbuf, i)
            i += 1
        return wv

    def front(self, xsrc, xsrc_b, rows, dst=None, col0=0):
        S = self.S
        xt = self.xt.next()
        S.dma("sp", xt.t[:], xsrc, reads=[xsrc_b], writes=[xt])
        sm = self.sm.next()
        S.op("act", lambda e: e.activation(out=self.junkB.t[:], in_=xt.t[:], func=AF.Square, accum_out=sm.t[:, 0:1]),
             reads=[xt], writes=[sm, self.junkB])
        S.op("dve", lambda e: e.tensor_scalar(out=sm.t[:, 1:2], in0=sm.t[:, 0:1], scalar1=1.0 / 1024.0, scalar2=EPS, op0=ALU.mult, op1=ALU.add), reads=[sm], writes=[sm])
        S.op("act", lambda e: e.activation(out=sm.t[:, 2:3], in_=sm.t[:, 1:2], func=AF.Sqrt), reads=[sm], writes=[sm])
        S.op("dve", lambda e: e.reciprocal(out=sm.t[:, 3:4], in_=sm.t[:, 2:3]), reads=[sm], writes=[sm])
        t32 = self.t32.next()
        S.op("dve", lambda e: e.scalar_tensor_tensor(out=t32.t[:], in0=xt.t[:], scalar=sm.t[:, 3:4], in1=rows.t[:, 1, :], op0=ALU.mult, op1=ALU.mult),
             reads=[xt, sm, rows], writes=[t32])
        hb = self.hb.next()
        S.op("pool", lambda e: e.tensor_tensor(out=hb.t[:], in0=t32.t[:], in1=rows.t[:, 0, :], op=ALU.add), reads=[t32, rows], writes=[hb])
        if dst is not None:
            self.transpose8(hb, dst, col0=col0)
            return dst
        hT = self.hT.next()
        self.transpose8(hb, hT)
        return hT

    def transpose8(self, src_t, dstT, nk=8, col0=None):
        S = self.S
        srcbuf = src_t
        src_t = srcbuf.t
        pt = self.pt.next()
        ptv = pt.t[:].bitcast(BF16)
        for kc in range(nk):
            S.op("pe", lambda e, kc=kc: e.transpose(ptv[:, kc * 128:(kc + 1) * 128], src_t[:, kc * 128:(kc + 1) * 128], self.ident.t[:]),
                 reads=[srcbuf, self.ident], writes=[pt], acc=(kc > 0))
        if col0 is not None:
            S.op("act", lambda e: e.copy(out=dstT.t[:, :, col0:col0 + 128], in_=ptv[:, 0:nk * 128].rearrange("p (k n) -> p k n", k=nk)), reads=[pt], writes=[dstT])
            return
        S.op("act", lambda e: e.copy(out=dstT.t[:].rearrange("p k n -> p (k n)")[:, 0:nk * 128], in_=ptv[:, 0:nk * 128]), reads=[pt], writes=[dstT])

    def back(self, ypA, ypB, rows, xsrc, xsrc_b, xdst, xdst_b):
        S = self.S
        sm = self.sm.next()
        for j, yp in enumerate((ypA, ypB)):
            S.op("act", lambda e, j=j, yp=yp: e.activation(out=self.junkB.t[:, j * 512:(j + 1) * 512], in_=yp.t[:], func=AF.Square, accum_out=sm.t[:, j:j + 1]),
                 reads=[yp], writes=[sm, self.junkB])
        S.op("dve", lambda e: e.tensor_tensor(out=sm.t[:, 2:3], in0=sm.t[:, 0:1], in1=sm.t[:, 1:2], op=ALU.add), reads=[sm], writes=[sm])
        S.op("dve", lambda e: e.tensor_scalar(out=sm.t[:, 3:4], in0=sm.t[:, 2:3], scalar1=1.0 / 1024.0, scalar2=EPS, op0=ALU.mult, op1=ALU.add), reads=[sm], writes=[sm])
        S.op("act", lambda e: e.activation(out=sm.t[:, 4:5], in_=sm.t[:, 3:4], func=AF.Sqrt), reads=[sm], writes=[sm])
        S.op("dve", lambda e: e.reciprocal(out=sm.t[:, 5:6], in_=sm.t[:, 4:5]), reads=[sm], writes=[sm])
        t32 = self.t32.next()
        for j, yp in enumerate((ypA, ypB)):
            S.op("dve", lambda e, j=j, yp=yp: e.scalar_tensor_tensor(out=t32.t[:, j * 512:(j + 1) * 512], in0=yp.t[:], scalar=sm.t[:, 5:6], in1=rows.t[:, 2, j * 512:(j + 1) * 512], op0=ALU.mult, op1=ALU.mult),
                 reads=[yp, sm, rows], writes=[t32])
        xt = self.xt.next()
        S.dma("sp", xt.t[:], xsrc, reads=[xsrc_b], writes=[xt])
        S.op("pool", lambda e: e.tensor_tensor(out=xt.t[:], in0=xt.t[:], in1=t32.t[:], op=ALU.add), reads=[xt, t32], writes=[xt])
        S.dma("sp", xdst, xt.t[:], reads=[xt], writes=[xdst_b], key=xt.name + "_st")

    def outproj(self, uT, wout, nk=8):
        S = self.S
        yps = [self.pb.next(), self.pb.next()]
        for j in range(2):
            for kc in range(nk):
                S.op("pe", lambda e, j=j, kc=kc: e.matmul(yps[j].t[:], lhsT=uT.t[:, kc, :], rhs=wout[:, kc, j * 512:(j + 1) * 512], start=(kc == 0), stop=(kc == nk - 1)),
                     reads=[uT, self.wbuf], writes=[yps[j]], acc=(kc > 0))
        return yps
    def gla_alloc(self):
        S = self.S
        A = dict(arena=True)
        L = f"L{self.cur_l}"
        self.wg2 = S.carve(L + "wg2", [17, 2, 512], BF16)
        self.ghead = S.carve(L + "ghead", [128, 1024], F32)
        self.rTa = [Rot(S, L + f"rTa{dd}_", [17, 128], BF16, 2, **A) for dd in range(2)]
        self.sp32 = [Rot(S, L + f"sp{dd}_", [128, 512], F32, 2, **A) for dd in range(2)]
        self.ex = Rot(S, L + "ex", [128, 512], F32, 1, **A)
        self.efac = Rot(S, L + "efac", [128, 512], BF16, 6, **A)
        self.qk = Rot(S, L + "qk", [128, 512], BF16, 6, **A)
        self.qkT = Rot(S, L + "qkT", [128, 4, 128], BF16, 6, **A)
        self.vb = Rot(S, L + "vb", [128, 1024], BF16, 2, **A)
        self.szb = Rot(S, L + "szb", [128, 1024], BF16, 1, **A)
        self.scT = Rot(S, L + "scT", [128, 4, 128], BF16, 2, **A)
        self.Sf = S.carve(L + "Sf", [128, 1024], F32)
        self.Sb = S.carve(L + "Sb", [128, 1024], F32)
        self.dSf = self.Sf
        self.dSb = self.Sb
        self.Sbf = Rot(S, L + "Sbf", [128, 1024], BF16, 4, **A)
        self.Dld = Rot(S, L + "Dld", [128, 1024], F32, 2, **A)
        self.dtf = S.carve(L + "dtf", [128, NT + NCT, 4], F32)
        self.dtb = S.carve(L + "dtb", [128, NT + NCT, 4], F32)
        self.Pf = S.carve(L + "Pf", [128, NT + 1, 4], F32)
        self.Pb = S.carve(L + "Pb", [128, NT + 1, 4], F32)
        self.ubufs = Rot(S, L + "ub", [128, 1024], BF16, 2, **A)
        self.uT = Rot(S, L + "uT", [128, 8, 128], BF16, 2, **A)
        for dd in range(2):
            for b in self.rTa[dd].bufs:
                S.op("pool", lambda e, b=b: e.memset(b.t[:], 1.0), writes=[b])

    def gla_gates(self, hT, full):
        S = self.S
        W = self.gW
        r = {}
        sps = []
        for dd in range(2):
            ps = self.pb.next()
            for kc in range(8):
                S.op("pe", lambda e, kc=kc, dd=dd, ps=ps: e.matmul(ps.t[0:16, 0:128], lhsT=W["g1"][:, kc, dd * 16:(dd + 1) * 16], rhs=hT.t[:, kc, :], start=(kc == 0), stop=(kc == 7)),
                     reads=[hT, self.wbuf], writes=[ps], acc=(kc > 0))
            rT = self.rTa[dd].next()
            S.op("act", lambda e, ps=ps, rT=rT: e.copy(out=rT.t[0:16, :], in_=ps.t[0:16, 0:128]), reads=[ps], writes=[rT])
            pg = self.pb.next()
            S.op("pe", lambda e, dd=dd, rT=rT, pg=pg: e.matmul(pg.t[:], lhsT=rT.t[:], rhs=self.wg2.t[:, dd, :], start=True, stop=True), reads=[rT, self.wg2], writes=[pg])
            ex = self.ex.next()
            S.op("act", lambda e, pg=pg, ex=ex: e.activation(out=ex.t[:], in_=pg.t[:], func=AF.Exp, scale=-1.0), reads=[pg], writes=[ex])
            sp = self.sp32[dd].next()
            S.op("act", lambda e, ex=ex, sp=sp: e.activation(out=sp.t[:], in_=ex.t[:], func=AF.Ln, bias=1.0), reads=[ex], writes=[sp])
            sps.append(sp)
        def cum_exp(sp, li, scale, name):
            ps = self.pb.next()
            S.op("pe", lambda e: e.matmul(ps.t[:], lhsT=self.lmats.t[:, li, :], rhs=sp.t[:], start=True, stop=True), reads=[self.lmats, sp], writes=[ps])
            outs = []
            for sc in scale:
                ef = self.efac.next()
                S.op("act", lambda e, ef=ef, sc=sc: e.activation(out=ef.t[:], in_=ps.t[:], func=AF.Exp, scale=sc), reads=[ps], writes=[ef])
                outs.append(ef)
            return outs
        r["ekh_f"], = cum_exp(sps[0], 2, [1.0], "remf")
        r["ekh_b"], = cum_exp(sps[1], 3, [1.0], "remb")
        if full:
            r["eq_f"], r["ekt_f"] = cum_exp(sps[0], 0, [1.0, -1.0], "bf")
            r["eq_b"], r["ekt_b"] = cum_exp(sps[1], 1, [1.0, -1.0], "cb")
        ps = self.pb.next()
        for dd in range(2):
            for hd in range(4):
                c = dd * 4 + hd
                S.op("pe", lambda e, dd=dd, hd=hd, c=c: e.matmul(ps.t[:, c:c + 1], lhsT=sps[dd].t[:, hd * 128:(hd + 1) * 128], rhs=self.negcol.t[:], start=True, stop=True),
                     reads=[sps[dd], self.negcol], writes=[ps], acc=(c > 0))
        r["dtot_ps"] = ps
        return r

    def gla_proj_tok(self, hT, c0, nb):
        S = self.S
        W = self.gW
        outs = []
        for j in range(nb):
            ps = self.pb.next()
            for kc in range(8):
                S.op("pe", lambda e, kc=kc, j=j, ps=ps: e.matmul(ps.t[:], lhsT=hT.t[:, kc, :], rhs=W["in"][:, kc, c0 + j * 512:c0 + (j + 1) * 512], start=(kc == 0), stop=(kc == 7)),
                     reads=[hT, self.wbuf], writes=[ps], acc=(kc > 0))
            outs.append(ps)
        return outs

    def gla_passA(self, tiles, rows, xsrc, xsrc_b, base):
        S = self.S
        def tile_a(n):
            gi = base + n
            hT = self.front(xsrc[n * 128:(n + 1) * 128, :], xsrc_b[n], rows)
            g = self.gla_gates(hT, full=False)
            kp, = self.gla_proj_tok(hT, 512, 1)
            kh = []
            for key in ("ekh_f", "ekh_b"):
                o = self.qk.next()
                S.op("dve", lambda e, o=o, key=key: e.tensor_tensor(out=o.t[:], in0=kp.t[:], in1=g[key].t[:], op=ALU.mult), reads=[kp, g[key]], writes=[o])
                kh.append(o)
            vps = self.gla_proj_tok(hT, 1024, 2)
            vb = self.vb.next()
            for j in range(2):
                S.op("act", lambda e, j=j: e.copy(out=vb.t[:, j * 512:(j + 1) * 512], in_=vps[j].t[:]), reads=[vps[j]], writes=[vb])
            S.op("act", lambda e, gi=gi: e.activation(out=self.dtf.t[:, gi, :], in_=g["dtot_ps"].t[:, 0:4], func=AF.Exp), reads=[g["dtot_ps"]], writes=[self.dtf])
            S.op("act", lambda e, gi=gi: e.activation(out=self.dtb.t[:, gi, :], in_=g["dtot_ps"].t[:, 4:8], func=AF.Exp), reads=[g["dtot_ps"]], writes=[self.dtb])
            sb = self.Sbf.next()
            S.op("pool", lambda e, sb=sb: e.tensor_copy(out=sb.t[:], in_=self.Sf.t[:]), reads=[self.Sf], writes=[sb])
            S.dma("sp", self.SfD[gi], sb.t[:], reads=[sb], writes=[self.SfD_b[gi]], key=sb.name + "_st")
            for dd in range(2):
                dps = [self.pb.next(), self.pb.next()]
                for hd in range(4):
                    S.op("pe", lambda e, dd=dd, hd=hd, dps=dps: e.matmul(dps[hd // 2].t[:, (hd % 2) * 256:(hd % 2 + 1) * 256], lhsT=kh[dd].t[:, hd * 128:(hd + 1) * 128], rhs=vb.t[:, hd * 256:(hd + 1) * 256], start=True, stop=True),
                         reads=[kh[dd], vb], writes=[dps[hd // 2]], acc=(hd % 2 == 1))
                if dd == 0:
                    for hd in range(4):
                        S.op("dve", lambda e, hd=hd, gi=gi, dps=dps: e.scalar_tensor_tensor(out=self.Sf.t[:, hd * 256:(hd + 1) * 256], in0=self.Sf.t[:, hd * 256:(hd + 1) * 256], scalar=self.dtf.t[:, gi, hd:hd + 1], in1=dps[hd // 2].t[:, (hd % 2) * 256:(hd % 2 + 1) * 256], op0=ALU.mult, op1=ALU.add),
                             reads=[self.Sf, self.dtf, dps[hd // 2]], writes=[self.Sf])
                else:
                    dl = self.Dld.next()
                    for j in range(2):
                        S.op("act", lambda e, j=j, dl=dl, dps=dps: e.copy(out=dl.t[:, j * 512:(j + 1) * 512], in_=dps[j].t[:]), reads=[dps[j]], writes=[dl])
                    S.dma("sp", self.DbD[gi], dl.t[:], reads=[dl], writes=[self.DbD_b[gi]], key=dl.name + "_st")
        for n in tiles:
            tile_a(n)

        def tile_r(n):
            gi = base + n
            sb = self.Sbf.next()
            S.op("pool", lambda e, sb=sb: e.tensor_copy(out=sb.t[:], in_=self.Sb.t[:]), reads=[self.Sb], writes=[sb])
            S.dma("sp", self.SbD[gi], sb.t[:], reads=[sb], writes=[self.SbD_b[gi]], key=sb.name + "_st")
            dl = self.Dld.next()
            S.dma("sp", dl.t[:], self.DbD[gi], reads=[self.DbD_b[gi]], writes=[dl])
            for hd in range(4):
                S.op("dve", lambda e, hd=hd, gi=gi, dl=dl: e.scalar_tensor_tensor(out=self.Sb.t[:, hd * 256:(hd + 1) * 256], in0=self.Sb.t[:, hd * 256:(hd + 1) * 256], scalar=self.dtb.t[:, gi, hd:hd + 1], in1=dl.t[:, hd * 256:(hd + 1) * 256], op0=ALU.mult, op1=ALU.add),
                     reads=[self.Sb, self.dtb, dl], writes=[self.Sb])
        for n in reversed(tiles):
            tile_r(n)

    def gla_passB(self, tiles, rows, xsrc, xsrc_b, xdst, xdst_b, base, delta):
        S = self.S
        W = self.gW
        def tile_b(n):
            gi = base + n
            hT = self.front(xsrc[n * 128:(n + 1) * 128, :], xsrc_b[n], rows)
            g = self.gla_gates(hT, full=True)
            st = []
            for dd, (SD, SD_b, dS, P) in enumerate(((self.SfD, self.SfD_b, self.dSf, self.Pf), (self.SbD, self.SbD_b, self.dSb, self.Pb))):
                sb = self.Sbf.next()
                S.dma("sp", sb.t[:], SD[gi], reads=[SD_b[gi]], writes=[sb])
                if delta:
                    for hd in range(4):
                        S.op("dve", lambda e, hd=hd, sb=sb, dS=dS, P=P, n=n: e.scalar_tensor_tensor(out=sb.t[:, hd * 256:(hd + 1) * 256], in0=dS.t[:, hd * 256:(hd + 1) * 256], scalar=P.t[:, n, hd:hd + 1], in1=sb.t[:, hd * 256:(hd + 1) * 256], op0=ALU.mult, op1=ALU.add),
                             reads=[dS, P, sb], writes=[sb])
                st.append(sb)
            qp, kp = self.gla_proj_tok(hT, 0, 2)
            tT = {}
            for nm, src, key, scl in (("qf", qp, "eq_f", 128.0 ** -0.5), ("qb", qp, "eq_b", 128.0 ** -0.5), ("kf", kp, "ekt_f", 1.0), ("kb", kp, "ekt_b", 1.0)):
                o = self.qk.next()
                S.op("dve", lambda e, o=o, src=src, key=key, scl=scl: e.scalar_tensor_tensor(out=o.t[:], in0=src.t[:], scalar=scl, in1=g[key].t[:], op0=ALU.mult, op1=ALU.mult),
                     reads=[src, g[key]], writes=[o])
                pt = self.pt.next()
                ptv = pt.t[:].bitcast(BF16)
                for hd in range(4):
                    S.op("pe", lambda e, hd=hd, o=o, ptv=ptv: e.transpose(ptv[:, hd * 128:(hd + 1) * 128], o.t[:, hd * 128:(hd + 1) * 128], self.ident.t[:]),
                         reads=[o, self.ident], writes=[pt], acc=(hd > 0))
                oT = self.qkT.next()
                S.op("act", lambda e, oT=oT, ptv=ptv: e.copy(out=oT.t[:].rearrange("p k n -> p (k n)"), in_=ptv[:, 0:512]), reads=[pt], writes=[oT])
                tT[nm] = oT
            vps = self.gla_proj_tok(hT, 1024, 2)
            vb = self.vb.next()
            for j in range(2):
                S.op("act", lambda e, j=j: e.copy(out=vb.t[:, j * 512:(j + 1) * 512], in_=vps[j].t[:]), reads=[vps[j]], writes=[vb])
            zps = self.gla_proj_tok(hT, 2048, 2)
            sz = self.szb.next()
            for j in range(2):
                S.op("act", lambda e, j=j: e.activation(out=sz.t[:, j * 512:(j + 1) * 512], in_=zps[j].t[:], func=AF.Silu), reads=[zps[j]], writes=[sz])
            scs = []
            for dd, (kn, qn) in enumerate((("kf", "qf"), ("kb", "qb"))):
                ps = self.pb.next()
                for hd in range(4):
                    S.op("pe", lambda e, hd=hd, kn=kn, qn=qn, ps=ps: e.matmul(ps.t[:, hd * 128:(hd + 1) * 128], lhsT=tT[kn].t[:, hd, :], rhs=tT[qn].t[:, hd, :], start=True, stop=True),
                         reads=[tT[kn], tT[qn]], writes=[ps], acc=(hd > 0))
                sc = self.scT.next()
                S.op("dve", lambda e, dd=dd, ps=ps, sc=sc: e.tensor_tensor(out=sc.t[:].rearrange("p h n -> p (h n)"), in0=ps.t[:], in1=self.masks.t[:, dd].rearrange("p h n -> p (h n)"), op=ALU.mult),
                     reads=[ps, self.masks], writes=[sc])
                scs.append(sc)
            ops_ = [self.pb.next(), self.pb.next()]
            for hd in range(4):
                dst = ops_[hd // 2].t[:, (hd % 2) * 256:(hd % 2 + 1) * 256]
                vs = vb.t[:, hd * 256:(hd + 1) * 256]
                terms = [(scs[0].t[:, hd, :], vs, [scs[0], vb]), (scs[1].t[:, hd, :], vs, [scs[1], vb]),
                         (tT["qf"].t[:, hd, :], st[0].t[:, hd * 256:(hd + 1) * 256], [tT["qf"], st[0]]),
                         (tT["qb"].t[:, hd, :], st[1].t[:, hd * 256:(hd + 1) * 256], [tT["qb"], st[1]])]
                for ti, (l_, r_, rd) in enumerate(terms):
                    S.op("pe", lambda e, dst=dst, l_=l_, r_=r_, ti=ti: e.matmul(dst, lhsT=l_, rhs=r_, start=(ti == 0), stop=(ti == 3)),
                         reads=rd, writes=[ops_[hd // 2]], acc=not (ti == 0 and hd % 2 == 0))
            sm = self.sm.next()
            for hd in range(4):
                S.op("act", lambda e, hd=hd: e.activation(out=self.junkB.t[:, hd * 256:(hd + 1) * 256], in_=ops_[hd // 2].t[:, (hd % 2) * 256:(hd % 2 + 1) * 256], func=AF.Square, accum_out=sm.t[:, hd:hd + 1]),
                     reads=[ops_[hd // 2]], writes=[sm, self.junkB])
            sm2 = self.sm.next()
            S.op("dve", lambda e: e.tensor_scalar(out=sm2.t[:, 0:4], in0=sm.t[:, 0:4], scalar1=1.0 / 256.0, scalar2=EPS, op0=ALU.mult, op1=ALU.add), reads=[sm], writes=[sm2])
            S.op("act", lambda e: e.activation(out=sm2.t[:, 4:8], in_=sm2.t[:, 0:4], func=AF.Sqrt), reads=[sm2], writes=[sm2])
            sm3 = self.sm.next()
            S.op("dve", lambda e: e.reciprocal(out=sm3.t[:, 0:4], in_=sm2.t[:, 4:8]), reads=[sm2], writes=[sm3])
            t32 = self.t32.next()
            for hd in range(4):
                S.op("dve", lambda e, hd=hd: e.scalar_tensor_tensor(out=t32.t[:, hd * 256:(hd + 1) * 256], in0=ops_[hd // 2].t[:, (hd % 2) * 256:(hd % 2 + 1) * 256], scalar=sm3.t[:, hd:hd + 1], in1=self.ghead.t[:, hd * 256:(hd + 1) * 256], op0=ALU.mult, op1=ALU.mult),
                     reads=[ops_[hd // 2], sm3, self.ghead], writes=[t32])
            ub = self.ubufs.next()
            S.op("pool", lambda e: e.tensor_tensor(out=ub.t[:], in0=t32.t[:], in1=sz.t[:], op=ALU.mult), reads=[t32, sz], writes=[ub])
            uT = self.uT.next()
            self.transpose8(ub, uT)
            yps = self.outproj(uT, W["out"])
            self.back(yps[0], yps[1], rows, xsrc[n * 128:(n + 1) * 128, :], xsrc_b[n], xdst[n * 128:(n + 1) * 128, :], xdst_b[n])
        for n in tiles:
            tile_b(n)

    def layer_gla(self, l, j, xin, xin_b, xcin, xcin_b, xout, xout_b, xcout, xcout_b, ctx_out):
        S, d = self.S, self.d
        self.gla_alloc()
        nt = NT + NCT
        self.SfD_b = [S.dram_buf(f"SfD{l}_{i}") for i in range(nt)]
        self.SbD_b = [S.dram_buf(f"SbD{l}_{i}") for i in range(nt)]
        self.DbD_b = [S.dram_buf(f"DbD{l}_{i}") for i in range(nt)]
        W = {}
        W["in"] = self.load_w(0, d["a_w_in"][j], 3072)
        W["out"] = self.load_w(24576, d["a_w_out"][j], 1024)
        g1v = self.wbuf.t[:, 32768:32768 + 256].rearrange("p (k n) -> p k n", k=8)
        for dd in range(2):
            self.load_cast(g1v[:, :, dd * 16:(dd + 1) * 16], d["a_w_g1"][j, dd].rearrange("(k p) n -> p k n", p=128), [128, 8, 16], self.wbuf, dd)
        W["g1"] = g1v
        self.gW = W
        st = self.stg.next()
        stv = st.t[:, 0:1024].rearrange("p (a b) -> p a b", a=2)
        S.dma("sp", stv[0:16], d["a_w_g2"][j].rearrange("d r n -> r d n"), writes=[st])
        S.dma("sp", stv[16:17], d["a_b_g"][j:j + 1], writes=[st])
        S.op("dve", lambda e: e.tensor_copy(out=self.wg2.t[:], in_=stv[0:17]), reads=[st], writes=[self.wg2])
        S.dma("sp", self.ghead.t[:], d["a_g_head"][j].rearrange("h v -> (h v)").partition_broadcast(128), writes=[self.ghead])
        if self.stage <= 0:
            return
        S.op("pool", lambda e: e.memset(self.Sf.t[:], 0.0), writes=[self.Sf])
        S.op("pool", lambda e: e.memset(self.Sb.t[:], 0.0), writes=[self.Sb])
        self.gla_passA(list(range(NCT)), self.load_rows(l, 1), xcin, xcin_b, NT)
        if self.stage <= 1:
            return
        scb_ = S.dram_buf(f"sctx{l}")
        S.dma("sp", self.sctxD[0], self.Sf.t[:], reads=[self.Sf], writes=[scb_], key="sctx_st")
        S.dma("sp", self.sctxD[1], self.Sb.t[:], reads=[self.Sb], writes=[scb_], key="sctx_st")
        self.gla_passA(list(range(NT)), self.load_rows(l, 0), xin, xin_b, 0)
        if self.stage <= 2:
            return
        S.op("pool", lambda e: e.memset(self.Pf.t[:, 0, :], 1.0), writes=[self.Pf])
        S.op("pool", lambda e: e.memset(self.Pb.t[:, NT - 1, :], 1.0), writes=[self.Pb])
        for n in range(1, NT):
            S.op("dve", lambda e, n=n: e.tensor_tensor(out=self.Pf.t[:, n, :], in0=self.Pf.t[:, n - 1, :], in1=self.dtf.t[:, n - 1, :], op=ALU.mult), reads=[self.Pf, self.dtf], writes=[self.Pf])
        for n in range(NT - 2, -1, -1):
            S.op("dve", lambda e, n=n: e.tensor_tensor(out=self.Pb.t[:, n, :], in0=self.Pb.t[:, n + 1, :], in1=self.dtb.t[:, n + 1, :], op=ALU.mult), reads=[self.Pb, self.dtb], writes=[self.Pb])
        if not hasattr(self, "bb"):
            self.bb, self.gb = S.dram_buf("bounce"), S.dram_buf("gath")
        bb, gb = self.bb, self.gb
        S.dma("pool", self.bounce.ap()[0:128, :], self.Sf.t[:], reads=[self.Sf], writes=[bb], key="bnc")
        S.dma("pool", self.bounce.ap()[128:256, :], self.Sb.t[:], reads=[self.Sb], writes=[bb], key="bnc")
        S.op("pool", lambda e: e.collective_compute("AllGather", ALU.bypass, replica_groups=[[0, 1], [2, 3], [4, 5], [6, 7]], ins=[self.bounce.ap()], outs=[self.gath.ap()]),
             reads=[bb], writes=[gb], dma=True, key="cc", inc=1)
        if self.stage <= 3:
            return
        S.dma("pool", self.dSf.t[:], self.gath.ap()[0:128, :], reads=[gb], writes=[self.dSf])
        S.dma("pool", self.dSb.t[:], self.gath.ap()[384:512, :], reads=[gb], writes=[self.dSb])
        for dS, fl in ((self.dSf, 1), (self.dSb, 0)):
            Sc = self.Dld.next()
            S.dma("sp", Sc.t[:], self.sctxD[1 - fl], reads=[scb_], writes=[Sc])
            S.op("dve", lambda e, dS=dS, Sc=Sc: e.tensor_tensor(out=dS.t[:], in0=dS.t[:], in1=Sc.t[:], op=ALU.subtract), reads=[dS, Sc], writes=[dS])
            S.op("dve", lambda e, dS=dS, fl=fl: e.tensor_scalar(out=dS.t[:], in0=dS.t[:], scalar1=self.flags.t[:, fl:fl + 1], scalar2=None, op0=ALU.mult), reads=[dS, self.flags], writes=[dS])
        if self.stage <= 4:
            return
        if ctx_out or self.stage == 5:
            self.gla_passB(list(range(NCT)), self.load_rows(l, 1), xcin, xcin_b, xcout, xcout_b, NT, delta=False)
        if self.stage <= 5:
            return
        if self.stage == 6:
            self.gla_passB(list(range(NT)), self.load_rows(l, 0), xin, xin_b, xout, xout_b, 0, delta=False)
            return
        if self.stage == 7:
            self.gla_passB([0, 1], self.load_rows(l, 0), xin, xin_b, xout, xout_b, 0, delta=True)
            return
        self.gla_passB(list(range(NT)), self.load_rows(l, 0), xin, xin_b, xout, xout_b, 0, delta=True)

    def layer_swa(self, l, xin, xin_b, xcin, xcin_b, xout, xout_b, xcout, xcout_b, ctx_out):
        S, d, nc = self.S, self.d, self.nc
        L = f"L{l}"
        A = dict(arena=True)
        Win = self.load_w(0, d["b_w_in"][0], 2560)
        Wout = self.load_w(20480, d["b_w_out"][0], 1024)
        ones128 = S.carve(L + "ones", [128, 128], BF16)
        S.op("pool", lambda e: e.memset(ones128.t[:], 1.0), writes=[ones128])
        sk = S.carve(L + "sk", [1, 16], F32)
        esrow = S.carve(L + "esrow", [1, 16, 128], BF16)
        S.dma("sp", sk.t[:], d["b_sink"][0:1, :], writes=[sk])
        S.op("act", lambda e: e.activation(out=sk.t[:], in_=sk.t[:], func=AF.Exp), reads=[sk], writes=[sk])
        for h in range(16):
            S.op("dve", lambda e, h=h: e.tensor_scalar(out=esrow.t[0:1, h, :], in0=ones128.t[0:1, :], scalar1=sk.t[0:1, h:h + 1], scalar2=None, op0=ALU.mult), reads=[ones128, sk], writes=[esrow])
        mL0 = S.carve(L + "mL0", [128, 4, 128], BF16)
        mR31 = S.carve(L + "mR31", [128, 4, 128], BF16)
        S.op("dve", lambda e: e.tensor_scalar(out=mL0.t[:], in0=self.masks.t[:, 1], scalar1=self.flags.t[:, 1:2], scalar2=None, op0=ALU.mult), reads=[self.masks, self.flags], writes=[mL0])
        S.op("dve", lambda e: e.tensor_scalar(out=mR31.t[:], in0=self.masks.t[:, 0], scalar1=self.flags.t[:, 0:1], scalar2=None, op0=ALU.mult), reads=[self.masks, self.flags], writes=[mR31])
        KT = Rot(S, L + "KT", [64, 4, 128], BF16, 4, **A)
        VD = Rot(S, L + "VD", [128, 4, 128], BF16, 4, **A)
        KTx = [S.carve(L + f"KTx{i}", [64, 4, 128], BF16) for i in range(4)]
        VDx = [S.carve(L + f"VDx{i}", [128, 4, 128], BF16) for i in range(4)]
        qT = Rot(S, L + "qT", [64, 16, 128], BF16, 2, **A)
        zT = Rot(S, L + "zT", [128, 8, 128], BF16, 2, **A)
        szb = Rot(S, L + "szb", [128, 1024], BF16, 1, **A)
        qr = Rot(S, L + "qr", [128, 1024], BF16, 2, **A)
        t12 = Rot(S, L + "t12", [128, 512], F32, 3, **A)
        rp = Rot(S, L + "rp", [128, 2, 64], F32, 2, **A)
        Pb = Rot(S, L + "Pb", [128, 512], BF16, 6, **A)
        uTt = Rot(S, L + "uTs", [128, 8, 128], BF16, 2, **A)
        bncb = self.bounce.ap().bitcast(BF16)
        gthb = self.gath.ap().bitcast(BF16)
        if not hasattr(self, "bb"):
            self.bb, self.gb = S.dram_buf("bounce"), S.dram_buf("gath")

        def proj_tok(hT, c0, ncols):
            ps = self.pb.next()
            for kc in range(8):
                S.op("pe", lambda e, kc=kc: e.matmul(ps.t[:, 0:ncols], lhsT=hT.t[:, kc, :], rhs=Win[:, kc, c0:c0 + ncols], start=(kc == 0), stop=(kc == 7)),
                     reads=[hT, self.wbuf], writes=[ps], acc=(kc > 0))
            return ps

        def rope(ps, nh, dst_ap, dst_buf, rt):
            n = nh * 64
            pv = ps.t[:, 0:n].rearrange("p (h d) -> p h d", h=nh)
            cs = rt.t[:, 0:1, :].to_broadcast([128, nh, 64])
            sn = rt.t[:, 1:2, :].to_broadcast([128, nh, 64])
            t1, t2 = t12.next(), t12.next()
            t1v = t1.t[:, 0:n].rearrange("p (h d) -> p h d", h=nh)
            S.op("dve", lambda e: e.tensor_tensor(out=t1v, in0=pv, in1=cs, op=ALU.mult), reads=[ps, rt], writes=[t1])
            p5 = ps.t[:, 0:n].rearrange("p (h b x e) -> p h b x e", h=nh, b=2, x=2)
            s5 = sn.rearrange("p h (b x e) -> p h b x e", b=2, x=2)
            o5 = t2.t[:, 0:n].rearrange("p (h b x e) -> p h b x e", h=nh, b=2, x=2)
            for x in range(2):
                S.op("dve", lambda e, x=x: e.tensor_tensor(out=o5[:, :, :, x, :], in0=p5[:, :, :, 1 - x, :], in1=s5[:, :, :, x, :], op=ALU.mult), reads=[ps, rt], writes=[t2])
            S.op("pool", lambda e: e.tensor_tensor(out=dst_ap, in0=t1.t[:, 0:n], in1=t2.t[:, 0:n], op=ALU.add), reads=[t1, t2], writes=[dst_buf])

        def kv(hT, n_glob, use_rope, kt, vd):
            kp = proj_tok(hT, 1024, 256)
            kr = qr.next()
            if use_rope:
                rt = rp.next()
                S.dma("sp", rt.t[:], d["rope"][n_glob * 128:(n_glob + 1) * 128], writes=[rt])
                rope(kp, 4, kr.t[:, 0:256], kr, rt)
            else:
                S.op("act", lambda e: e.copy(out=kr.t[:, 0:256], in_=kp.t[:, 0:256]), reads=[kp], writes=[kr])
            pt = self.pt.next()
            ptv = pt.t[:].bitcast(BF16)
            for h in range(4):
                S.op("pe", lambda e, h=h: e.transpose(ptv[0:64, h * 128:(h + 1) * 128], kr.t[:, h * 64:(h + 1) * 64], self.ident.t[:]), reads=[kr, self.ident], writes=[pt], acc=(h > 0))
            S.op("act", lambda e: e.copy(out=kt.t[:].rearrange("p h n -> p (h n)"), in_=ptv[0:64, 0:512]), reads=[pt], writes=[kt])
            vp = proj_tok(hT, 1280, 256)
            vdv = vd.t[:].rearrange("p h (u e) -> p h u e", u=2)
            vpv = vp.t[:, 0:256].rearrange("p (h e) -> p h e", h=4)
            S.op("act", lambda e: e.copy(out=vdv[:, :, 0, :], in_=vpv), reads=[vp], writes=[vd])
            S.op("dve", lambda e: e.tensor_copy(out=vdv[:, :, 1, :], in_=vpv), reads=[vp], writes=[vd])

        def attn(hT, n_glob, use_rope, chunks, rows, xs_ap, xs_b, xd_ap, xd_b):
            q_t = qT.next()
            for half in range(2):
                qp = proj_tok(hT, half * 512, 512)
                qh = qr.next()
                if use_rope:
                    rt = rp.next()
                    S.dma("sp", rt.t[:], d["rope"][n_glob * 128:(n_glob + 1) * 128], writes=[rt])
                    rope(qp, 8, qh.t[:, 0:512], qh, rt)
                else:
                    S.op("act", lambda e, qp=qp, qh=qh: e.copy(out=qh.t[:, 0:512], in_=qp.t[:]), reads=[qp], writes=[qh])
                pt = self.pt.next()
                ptv = pt.t[:].bitcast(BF16)
                for h in range(8):
                    S.op("pe", lambda e, h=h, qh=qh, ptv=ptv: e.transpose(ptv[0:64, h * 128:(h + 1) * 128], qh.t[:, h * 64:(h + 1) * 64], self.ident.t[:]), reads=[qh, self.ident], writes=[pt], acc=(h > 0))
                S.op("act", lambda e, half=half, ptv=ptv: e.copy(out=q_t.t[:, half * 8:(half + 1) * 8, :].rearrange("p h n -> p (h n)"), in_=ptv[0:64, 0:1024]), reads=[pt], writes=[q_t])
            sz = szb.next()
            for j in range(2):
                zp = proj_tok(hT, 1536 + j * 512, 512)
                S.op("act", lambda e, j=j, zp=zp: e.activation(out=sz.t[:, j * 512:(j + 1) * 512], in_=zp.t[:], func=AF.Silu), reads=[zp], writes=[sz])
            z_t = zT.next()
            self.transpose8(sz, z_t)
            uT = uTt.next()
            stc = [0]
            for g in range(4):
                po, pd = self.pb.bufs[0], self.pb.bufs[1]
                S.op("pe", lambda e, g=g, pd=pd: e.matmul(pd.t[:], lhsT=ones128.t[0:1, :], rhs=esrow.t[0:1, 4 * g:4 * g + 4, :].rearrange("p h n -> p (h n)"), start=True, stop=False),
                     reads=[ones128, esrow], writes=[pd])
                for ci, (kt, vd, mk, mkb) in enumerate(chunks):
                    ps = self.pb.bufs[2 + stc[0] % 4]
                    stc[0] += 1
                    S.op("pe", lambda e, g=g, kt=kt, ps=ps: e.matmul(ps.t[:], lhsT=kt.t[:, g, :], rhs=q_t.t[:, 4 * g:4 * g + 4, :].rearrange("p h n -> p (h n)"), start=True, stop=True),
                         reads=[kt, q_t], writes=[ps])
                    P = Pb.next()
                    S.op("act", lambda e, ps=ps, P=P: e.activation(out=P.t[:], in_=ps.t[:], func=AF.Exp, scale=0.125), reads=[ps], writes=[P])
                    if mk is not None:
                        S.op("pool", lambda e, P=P, mk=mk: e.tensor_tensor(out=P.t[:], in0=P.t[:], in1=mk, op=ALU.mult), reads=[P, mkb], writes=[P])
                    last = (ci == len(chunks) - 1)
                    S.op("pe", lambda e, g=g, vd=vd, P=P, ci=ci, last=last, po=po: e.matmul(po.t[:], lhsT=vd.t[:, g, :], rhs=P.t[:], start=(ci == 0), stop=last),
                         reads=[vd, P], writes=[po], acc=(ci > 0))
                    S.op("pe", lambda e, P=P, last=last, pd=pd: e.matmul(pd.t[:], lhsT=ones128.t[:], rhs=P.t[:], start=False, stop=last),
                         reads=[ones128, P], writes=[pd], acc=True)
                rec = t12.next()
                S.op("dve", lambda e, rec=rec, pd=pd: e.reciprocal(out=rec.t[:], in_=pd.t[:]), reads=[pd], writes=[rec])
                S.op("dve", lambda e, rec=rec, po=po: e.tensor_tensor(out=rec.t[:], in0=po.t[:], in1=rec.t[:], op=ALU.mult), reads=[po, rec], writes=[rec])
                for i in range(4):
                    r0 = (i % 2) * 64
                    kc = 2 * g + i // 2
                    S.op("pool", lambda e, i=i, r0=r0, kc=kc, rec=rec: e.tensor_tensor(out=uT.t[r0:r0 + 64, kc, :], in0=rec.t[r0:r0 + 64, i * 128:(i + 1) * 128], in1=z_t.t[r0:r0 + 64, kc, :], op=ALU.mult),
                         reads=[rec, z_t], writes=[uT])
            yps = self.outproj(uT, Wout)
            self.back(yps[0], yps[1], rows, xs_ap, xs_b, xd_ap, xd_b)

        rows = self.load_rows(l, 1)
        hTc = []
        for n in range(NCT):
            hT = self.front(xcin[n * 128:(n + 1) * 128, :], xcin_b[n], rows)
            kv(hT, 0, False, KTx[n], VDx[n])
            if ctx_out:
                hTc.append(hT)
        if ctx_out:
            cch = [(KTx[0], VDx[0], None, None), (KTx[1], VDx[1], None, None)]
            for n in range(NCT):
                attn(hTc[n], 0, False, cch, rows, xcin[n * 128:(n + 1) * 128, :], xcin_b[n], xcout[n * 128:(n + 1) * 128, :], xcout_b[n])
        rows = self.load_rows(l, 0)
        for slot, n in ((0, 0), (1, NT - 1)):
            hT = self.front(xin[n * 128:(n + 1) * 128, :], xin_b[n], rows)
            kt, vd = KT.next(), VD.next()
            kv(hT, n, True, kt, vd)
            S.dma("pool", bncb[0:128, slot * 512:(slot + 1) * 512], vd.t[:].rearrange("p h n -> p (h n)"), reads=[vd], writes=[self.bb], key="bnc")
            S.dma("pool", bncb[0:64, 1024 + slot * 512:1024 + (slot + 1) * 512], kt.t[:].rearrange("p h n -> p (h n)"), reads=[kt], writes=[self.bb], key="bnc")
        S.op("pool", lambda e: e.collective_compute("AllGather", ALU.bypass, replica_groups=[[0, 1], [2, 3], [4, 5], [6, 7]], ins=[self.bounce.ap()], outs=[self.gath.ap()]),
             reads=[self.bb], writes=[self.gb], dma=True, key="cc", inc=1)
        S.dma("pool", VDx[2].t[:].rearrange("p h n -> p (h n)"), gthb[0:128, 512:1024], reads=[self.gb], writes=[VDx[2]])
        S.dma("pool", KTx[2].t[:].rearrange("p h n -> p (h n)"), gthb[0:64, 1536:2048], reads=[self.gb], writes=[KTx[2]])
        S.dma("pool", VDx[3].t[:].rearrange("p h n -> p (h n)"), gthb[256:384, 0:512], reads=[self.gb], writes=[VDx[3]])
        S.dma("pool", KTx[3].t[:].rearrange("p h n -> p (h n)"), gthb[256:320, 1024:1536], reads=[self.gb], writes=[KTx[3]])
        mLv = self.masks.t[:, 1].rearrange("p h n -> p (h n)")
        mRv = self.masks.t[:, 0].rearrange("p h n -> p (h n)")
        win = {}
        hTs = {}
        hTs[0] = self.front(xin[0:128, :], xin_b[0], rows)
        win[0] = (KT.next(), VD.next())
        kv(hTs[0], 0, True, *win[0])
        for n in range(NT):
            if n + 1 < NT:
                hTs[n + 1] = self.front(xin[(n + 1) * 128:(n + 2) * 128, :], xin_b[n + 1], rows)
                win[n + 1] = (KT.next(), VD.next())
                kv(hTs[n + 1], n + 1, True, *win[n + 1])
            if n == 0:
                left = (KTx[2], VDx[2], mL0.t[:].rearrange("p h n -> p (h n)"), mL0)
            else:
                left = (win[n - 1][0], win[n - 1][1], mLv, self.masks)
            if n == NT - 1:
                right = (KTx[3], VDx[3], mR31.t[:].rearrange("p h n -> p (h n)"), mR31)
            else:
                right = (win[n + 1][0], win[n + 1][1], mRv, self.masks)
            chunks = [left, (win[n][0], win[n][1], None, None), right, (KTx[0], VDx[0], None, None), (KTx[1], VDx[1], None, None)]
            attn(hTs[n], n, True, chunks, rows, xin[n * 128:(n + 1) * 128, :], xin_b[n], xout[n * 128:(n + 1) * 128, :], xout_b[n])


    def layer_lru(self, l, xin, xin_b, xcin, xcin_b, xout, xout_b, xcout, xcout_b, ctx_out):
        S, d, nc = self.S, self.d, self.nc
        L = f"L{l}"
        NB = NT // 4
        Win = self.load_w(0, d["c_w_in"][0], 2560)
        gv = self.wbuf.t[:, 20480:20480 + 5120].rearrange("p (a h j) -> p a h j", a=4, h=10)
        i_ = 0
        for dd in range(2):
            for gi_, nm in enumerate(("c_w_ra", "c_w_ri")):
                for h0 in (0, 8):
                    h1 = min(10, h0 + 8)
                    self.load_cast(gv[:, dd * 2 + gi_, h0:h1, :], d[nm][0, dd, h0:h1].rearrange("h i j -> i h j"), [128, h1 - h0, 128], self.wbuf, i_)
                    i_ += 1
        Wout = self.load_w(25600, d["c_w_out"][0], 1024, nk=10)
        vec = S.carve(L + "vec", [128, 11, 10], F32)
        S.dma("sp", vec.t[:], d["c_vecs"], writes=[vec])
        nlam = S.carve(L + "nlam", [128, 2, 2, 10], F32)
        tmpv = S.carve(L + "tmpv", [128, 2, 10], F32)
        S.op("act", lambda e: e.activation(out=tmpv.t[:], in_=vec.t[:, 9:11, :], func=AF.Exp, scale=-1.0), reads=[vec], writes=[tmpv])
        S.op("act", lambda e: e.activation(out=tmpv.t[:], in_=tmpv.t[:], func=AF.Ln, bias=1.0), reads=[tmpv], writes=[tmpv])
        S.op("dve", lambda e: e.tensor_scalar(out=nlam.t[:, 0], in0=tmpv.t[:], scalar1=-8.0, scalar2=None, op0=ALU.mult), reads=[tmpv], writes=[nlam])
        S.op("dve", lambda e: e.tensor_scalar(out=nlam.t[:, 1], in0=tmpv.t[:], scalar1=-16.0, scalar2=None, op0=ALU.mult), reads=[tmpv], writes=[nlam])
        hTb = S.carve(L + "hTb", [128, 8, 512], BF16)
        ue = S.carve(L + "ue", [128, 10, 515], F32)
        ucb = S.carve(L + "ucb", [128, 10, 512], BF16)
        hsum = S.carve(L + "hsum", [128, 10, 512], F32)
        zs = S.carve(L + "zs", [128, 10, 512], BF16)
        tmp = Rot(S, L + "tmp", [128, 512], F32, 5, arena=True)
        uTt = Rot(S, L + "uTt", [128, 10, 128], BF16, 2, arena=True)
        AF_ = S.carve(L + "AF", [128, (NB + 1) * 2, 2, 10], F32)
        Hin = S.carve(L + "Hin", [128, NB + 1, 2, 10], F32)
        Hc = S.carve(L + "Hc", [128, 2, 10], F32)
        Hrun = S.carve(L + "Hrun", [128, 2, 10], F32)
        halo = S.carve(L + "halo", [128, 10, 3], F32)
        Ud = nc.dram_tensor(L + "Ud", [128, 10, T_CORE + 3], F32).ap()
        Zd = nc.dram_tensor(L + "Zd", [128, 10, T_CORE], BF16).ap()
        Ucd = nc.dram_tensor(L + "Ucd", [128, 10, 256 + 3], F32).ap()
        Zcd = nc.dram_tensor(L + "Zcd", [128, 10, 256], BF16).ap()
        Ud_b = [S.dram_buf(L + f"Ud{i}") for i in range(NB)]
        Zd_b = [S.dram_buf(L + f"Zd{i}") for i in range(NB)]
        Uh_b = S.dram_buf(L + "Uhalo")
        Uc_b, Zc_b = S.dram_buf(L + "Ucd"), S.dram_buf(L + "Zcd")
        zero3 = S.carve(L + "zero3", [128, 10, 3], F32)
        S.op("pool", lambda e: e.memset(zero3.t[:], 0.0), writes=[zero3])

        def pass0(ntile, rows, xs, xs_b, Udst, Udst_b, Zdst, Zdst_b, blk):
            nb = ntile * 128
            for j in range(ntile):
                n = blk * 4 + j
                self.front(xs[n * 128:(n + 1) * 128, :], xs_b[n], rows, dst=hTb, col0=j * 128)
            for h in range(20):
                ps = self.pb.next()
                for kc in range(8):
                    S.op("pe", lambda e, h=h, kc=kc, ps=ps: e.matmul(ps.t[:, 0:nb], lhsT=Win[:, kc, h * 128:(h + 1) * 128], rhs=hTb.t[:, kc, 0:nb], start=(kc == 0), stop=(kc == 7)),
                         reads=[hTb, self.wbuf], writes=[ps], acc=(kc > 0))
                if h < 10:
                    S.op("dve", lambda e, h=h, ps=ps: e.tensor_copy(out=ue.t[:, h, 0:nb], in_=ps.t[:, 0:nb]), reads=[ps], writes=[ue])
                else:
                    S.op("act", lambda e, h=h, ps=ps: e.activation(out=zs.t[:, h - 10, 0:nb], in_=ps.t[:, 0:nb], func=AF.Silu), reads=[ps], writes=[zs])
            S.dma("sp", Udst[:, :, 2 + blk * 512:2 + blk * 512 + nb], ue.t[:, :, 0:nb], reads=[ue], writes=[Udst_b], key="ue_st")
            S.dma("sp", Zdst[:, :, blk * 512:blk * 512 + nb], zs.t[:, :, 0:nb], reads=[zs], writes=[Zdst_b], key="zs_st")

        def core(nb, Usrc, Usrc_b, blk, want_sum, cbk):
            S.dma("sp", ue.t[:, :, 0:nb + 3], Usrc[:, :, blk * 512:blk * 512 + nb + 3], reads=Usrc_b, writes=[ue])
            for h in range(10):
                t0 = tmp.next()
                eng = "dve" if h % 2 == 0 else "pool"
                S.op(eng, lambda e, h=h, t0=t0: e.tensor_scalar(out=t0.t[:, 0:nb], in0=ue.t[:, h, 0:nb], scalar1=vec.t[:, 0, h:h + 1], scalar2=vec.t[:, 4, h:h + 1], op0=ALU.mult, op1=ALU.add),
                     reads=[ue, vec], writes=[t0])
                for j in range(1, 4):
                    out_ap = (t0.t[:, 0:nb] if j < 3 else ucb.t[:, h, 0:nb])
                    S.op("dve", lambda e, h=h, t0=t0, j=j, out_ap=out_ap: e.scalar_tensor_tensor(out=out_ap, in0=ue.t[:, h, j:j + nb], scalar=vec.t[:, j, h:h + 1], in1=t0.t[:, 0:nb], op0=ALU.mult, op1=ALU.add),
                         reads=[ue, vec, t0], writes=[t0 if j < 3 else ucb])
            for dd in range(2):
                for h in range(10):
                    pr, pi = self.pb.next(), self.pb.next()
                    for g_, ps in ((0, pr), (1, pi)):
                        S.op("pe", lambda e, g_=g_, ps=ps, dd=dd, h=h: e.matmul(ps.t[:, 0:nb], lhsT=gv[:, dd * 2 + g_, h, :], rhs=ucb.t[:, h, 0:nb], start=True, stop=True),
                             reads=[ucb, self.wbuf], writes=[ps])
                    r = tmp.next()
                    sm = self.sm.next()
                    S.op("act", lambda e, r=r, pr=pr, dd=dd, h=h, sm=sm: e.activation(out=r.t[:, 0:nb], in_=pr.t[:, 0:nb], func=AF.Sigmoid, bias=vec.t[:, 5 + dd, h:h + 1], accum_out=sm.t[:, 0:1]),
                         reads=[pr, vec], writes=[r, sm])
                    ig = tmp.next()
                    S.op("act", lambda e, ig=ig, pi=pi, dd=dd, h=h: e.activation(out=ig.t[:, 0:nb], in_=pi.t[:, 0:nb], func=AF.Sigmoid, bias=vec.t[:, 7 + dd, h:h + 1]),
                         reads=[pi, vec], writes=[ig])
                    a = tmp.next()
                    S.op("act", lambda e, a=a, r=r, dd=dd, h=h: e.activation(out=a.t[:, 0:nb], in_=r.t[:, 0:nb], func=AF.Exp, scale=nlam.t[:, 0, dd, h:h + 1]), reads=[r, nlam], writes=[a])
                    S.op("act", lambda e, r=r, dd=dd, h=h: e.activation(out=r.t[:, 0:nb], in_=r.t[:, 0:nb], func=AF.Exp, scale=nlam.t[:, 1, dd, h:h + 1]), reads=[r, nlam], writes=[r])
                    S.op("pool", lambda e, r=r: e.tensor_scalar(out=r.t[:, 0:nb], in0=r.t[:, 0:nb], scalar1=-1.0, scalar2=1.0, op0=ALU.mult, op1=ALU.add), reads=[r], writes=[r])
                    S.op("act", lambda e, r=r: e.activation(out=r.t[:, 0:nb], in_=r.t[:, 0:nb], func=AF.Sqrt), reads=[r], writes=[r])
                    S.op("pool", lambda e, r=r, ig=ig: e.tensor_tensor(out=ig.t[:, 0:nb], in0=ig.t[:, 0:nb], in1=r.t[:, 0:nb], op=ALU.mult), reads=[ig, r], writes=[ig])
                    S.op("dve", lambda e, ig=ig, h=h: e.tensor_tensor(out=ig.t[:, 0:nb], in0=ig.t[:, 0:nb], in1=ucb.t[:, h, 0:nb], op=ALU.mult), reads=[ig, ucb], writes=[ig])
                    cbk(dd, h, a, ig, sm, nb)

        def summarize(nb, Usrc, Usrc_b, blk, slot):
            def cbk(dd, h, a, xin, sm, nb):
                hs = tmp.next()
                if dd == 0:
                    S.op("dve", lambda e: e.tensor_tensor_scan(out=hs.t[:, 0:nb], data0=a.t[:, 0:nb], data1=xin.t[:, 0:nb], initial=0.0, op0=ALU.mult, op1=ALU.add), reads=[a, xin], writes=[hs])
                    src = hs.t[:, nb - 1:nb]
                else:
                    S.op("dve", lambda e: e.tensor_tensor_scan(out=hs.t[:, 0:nb][:, ::-1], data0=a.t[:, 0:nb][:, ::-1], data1=xin.t[:, 0:nb][:, ::-1], initial=0.0, op0=ALU.mult, op1=ALU.add), reads=[a, xin], writes=[hs])
                    src = hs.t[:, 0:1]
                S.op("pool", lambda e: e.tensor_copy(out=AF_.t[:, slot * 2 + dd, 1, h:h + 1], in_=src), reads=[hs], writes=[AF_])
                S.op("act", lambda e: e.activation(out=AF_.t[:, slot * 2 + dd, 0, h:h + 1], in_=sm.t[:, 0:1], func=AF.Exp, scale=nlam.t[:, 0, dd, h:h + 1]), reads=[sm, nlam], writes=[AF_])
            core(nb, Usrc, Usrc_b, blk, True, cbk)

        def finish_block(nb, Usrc, Usrc_b, Zsrc, Zsrc_b, blk, slot, rows, xs, xs_b, xd, xd_b):
            def cbk(dd, h, a, xin, sm, nb):
                if dd == 0:
                    S.op("dve", lambda e: e.tensor_tensor_scan(out=hsum.t[:, h, 0:nb], data0=a.t[:, 0:nb], data1=xin.t[:, 0:nb], initial=Hin.t[:, slot, 0, h:h + 1], op0=ALU.mult, op1=ALU.add),
                         reads=[a, xin, Hin], writes=[hsum])
                else:
                    hs = tmp.next()
                    S.op("dve", lambda e: e.tensor_tensor_scan(out=hs.t[:, 0:nb][:, ::-1], data0=a.t[:, 0:nb][:, ::-1], data1=xin.t[:, 0:nb][:, ::-1], initial=Hin.t[:, slot, 1, h:h + 1], op0=ALU.mult, op1=ALU.add),
                         reads=[a, xin, Hin], writes=[hs])
                    S.op("pool", lambda e: e.tensor_tensor(out=hsum.t[:, h, 0:nb], in0=hsum.t[:, h, 0:nb], in1=hs.t[:, 0:nb], op=ALU.add), reads=[hsum, hs], writes=[hsum])
            core(nb, Usrc, Usrc_b, blk, False, cbk)
            S.dma("sp", zs.t[:, :, 0:nb], Zsrc[:, :, blk * 512:blk * 512 + nb], reads=Zsrc_b, writes=[zs])
            for j in range(nb // 128):
                n = blk * 4 + j
                uT = uTt.next()
                S.op("dve", lambda e, j=j, uT=uT: e.tensor_tensor(out=uT.t[:], in0=hsum.t[:, :, j * 128:(j + 1) * 128], in1=zs.t[:, :, j * 128:(j + 1) * 128], op=ALU.mult), reads=[hsum, zs], writes=[uT])
                yps = self.outproj(uT, Wout, nk=10)
                self.back(yps[0], yps[1], rows, xs[n * 128:(n + 1) * 128, :], xs_b[n], xd[n * 128:(n + 1) * 128, :], xd_b[n])

        def chain(slots, dd, init_ap):
            S.op("dve", lambda e: e.tensor_copy(out=Hrun.t[:, dd, :], in_=init_ap), reads=[Hc, Hin, halo], writes=[Hrun])
            for sl in slots:
                S.op("dve", lambda e, sl=sl: e.tensor_copy(out=Hin.t[:, sl, dd, :], in_=Hrun.t[:, dd, :]), reads=[Hrun], writes=[Hin])
                S.op("dve", lambda e, sl=sl: e.tensor_tensor(out=Hrun.t[:, dd, :], in0=Hrun.t[:, dd, :], in1=AF_.t[:, sl * 2 + dd, 0, :], op=ALU.mult), reads=[Hrun, AF_], writes=[Hrun])
                S.op("dve", lambda e, sl=sl: e.tensor_tensor(out=Hrun.t[:, dd, :], in0=Hrun.t[:, dd, :], in1=AF_.t[:, sl * 2 + dd, 1, :], op=ALU.add), reads=[Hrun, AF_], writes=[Hrun])

        rows = self.load_rows(l, 1)
        S.dma("sp", Ucd[:, :, 0:3], zero3.t[:], reads=[zero3], writes=[Uc_b], key="z3_st")
        S.dma("sp", Ucd[:, :, 256:259], zero3.t[:], reads=[zero3], writes=[Uc_b], key="z3_st")
        pass0(2, rows, xcin, xcin_b, Ucd, Uc_b, Zcd, Zc_b, 0)
        summarize(256, Ucd, [Uc_b], 0, NB)
        S.op("pool", lambda e: e.memset(Hc.t[:], 0.0), writes=[Hc])
        for dd in range(2):
            chain([NB], dd, Hc.t[:, dd, :])
        S.op("dve", lambda e: e.tensor_copy(out=Hc.t[:], in_=Hrun.t[:]), reads=[Hrun], writes=[Hc])
        if ctx_out:
            finish_block(256, Ucd, [Uc_b], Zcd, [Zc_b], 0, NB, rows, xcin, xcin_b, xcout, xcout_b)
        rows = self.load_rows(l, 0)
        for b in range(NB):
            pass0(4, rows, xin, xin_b, Ud, Ud_b[b], Zd, Zd_b[b], b)
        S.dma("sp", halo.t[:, :, 0:2], Ud[:, :, T_CORE:T_CORE + 2], reads=[Ud_b[NB - 1]], writes=[halo])
        S.dma("sp", halo.t[:, :, 2:3], Ud[:, :, 2:3], reads=[Ud_b[0]], writes=[halo])
        if not hasattr(self, "bb"):
            self.bb, self.gb = S.dram_buf("bounce"), S.dram_buf("gath")
        bnc = self.bounce.ap()
        gth = self.gath.ap()
        S.dma("pool", bnc[0:128, 0:30], halo.t[:].rearrange("p h t -> p (h t)"), reads=[halo], writes=[self.bb], key="bnc")
        S.op("pool", lambda e: e.collective_compute("AllGather", ALU.bypass, replica_groups=[[0, 1], [2, 3], [4, 5], [6, 7]], ins=[self.bounce.ap()], outs=[self.gath.ap()]),
             reads=[self.bb], writes=[self.gb], dma=True, key="cc", inc=1)
        hl = S.carve(L + "hl", [128, 10, 3], F32)
        hr = S.carve(L + "hr", [128, 10, 3], F32)
        S.dma("pool", hl.t[:].rearrange("p h t -> p (h t)"), gth[0:128, 0:30], reads=[self.gb], writes=[hl])
        S.dma("pool", hr.t[:].rearrange("p h t -> p (h t)"), gth[256:384, 0:30], reads=[self.gb], writes=[hr])
        S.op("dve", lambda e: e.tensor_scalar(out=hl.t[:], in0=hl.t[:], scalar1=self.flags.t[:, 1:2], scalar2=None, op0=ALU.mult), reads=[hl, self.flags], writes=[hl])
        S.op("dve", lambda e: e.tensor_scalar(out=hr.t[:], in0=hr.t[:], scalar1=self.flags.t[:, 0:1], scalar2=None, op0=ALU.mult), reads=[hr, self.flags], writes=[hr])
        S.dma("sp", Ud[:, :, 0:2], hl.t[:, :, 0:2], reads=[hl], writes=[Ud_b[0]], key="hl_st")
        S.dma("sp", Ud[:, :, T_CORE + 2:T_CORE + 3], hr.t[:, :, 2:3], reads=[hr], writes=[Ud_b[NB - 1]], key="hr_st")
        for b in range(NB):
            deps = [Ud_b[b]] + ([Ud_b[b - 1]] if b > 0 else []) + ([Ud_b[b + 1]] if b < NB - 1 else [])
            summarize(512, Ud, deps, b, b)
        chain(list(range(NB)), 0, Hc.t[:, 0, :])
        chain(list(range(NB - 1, -1, -1)), 1, Hc.t[:, 1, :])
        S.dma("pool", bnc[0:128, 0:20], Hrun.t[:].rearrange("p d h -> p (d h)"), reads=[Hrun], writes=[self.bb], key="bnc")
        S.op("pool", lambda e: e.collective_compute("AllGather", ALU.bypass, replica_groups=[[0, 1], [2, 3], [4, 5], [6, 7]], ins=[self.bounce.ap()], outs=[self.gath.ap()]),
             reads=[self.bb], writes=[self.gb], dma=True, key="cc", inc=1)
        pf = S.carve(L + "pf", [128, 2, 10], F32)
        pbk = S.carve(L + "pbk", [128, 2, 10], F32)
        S.dma("pool", pf.t[:].rearrange("p d h -> p (d h)"), gth[0:128, 0:20], reads=[self.gb], writes=[pf])
        S.dma("pool", pbk.t[:].rearrange("p d h -> p (d h)"), gth[256:384, 0:20], reads=[self.gb], writes=[pbk])
        ini = S.carve(L + "ini", [128, 2, 10], F32)
        for dd, (own_fl, par, par_fl) in enumerate(((0, pf, 1), (1, pbk, 0))):
            S.op("dve", lambda e, dd=dd, own_fl=own_fl: e.tensor_scalar(out=ini.t[:, dd, :], in0=Hc.t[:, dd, :], scalar1=self.flags.t[:, own_fl:own_fl + 1], scalar2=None, op0=ALU.mult), reads=[Hc, self.flags], writes=[ini])
            S.op("dve", lambda e, dd=dd, par=par, par_fl=par_fl: e.scalar_tensor_tensor(out=ini.t[:, dd, :], in0=par.t[:, dd, :], scalar=self.flags.t[:, par_fl:par_fl + 1], in1=ini.t[:, dd, :], op0=ALU.mult, op1=ALU.add),
                 reads=[par, self.flags, ini], writes=[ini])
        S.op("dve", lambda e: e.tensor_copy(out=Hc.t[:], in_=ini.t[:]), reads=[ini], writes=[Hc])
        chain(list(range(NB)), 0, Hc.t[:, 0, :])
        chain(list(range(NB - 1, -1, -1)), 1, Hc.t[:, 1, :])
        for b in range(NB):
            deps = [Ud_b[b]] + ([Ud_b[b - 1]] if b > 0 else []) + ([Ud_b[b + 1]] if b < NB - 1 else [])
            finish_block(512, Ud, deps, Zd, [Zd_b[b]], b, b, rows, xin, xin_b, xout, xout_b)


_W_NAMES = ["w_mod", "b_mod", "g_pre", "g_post", "a_w_in", "a_w_g1", "a_w_g2", "a_b_g", "a_g_head", "a_w_out",
            "b_w_in", "b_sink", "b_w_out", "c_w_in", "c_conv_w", "c_conv_b", "c_w_ra", "c_b_ra", "c_w_ri", "c_b_ri",
            "c_lam", "c_w_out"]


def _consts():
    import ml_dtypes
    s = np.arange(128)[:, None]
    t = np.arange(128)[None, :]
    c = {}
    c["ident"] = np.eye(128, dtype=np.float32).astype(ml_dtypes.bfloat16)
    lm = np.zeros((128, 4, 128), np.float32)
    lm[:, 0, :] = (s <= t)
    lm[:, 1, :] = (s >= t)
    lm[:, 2, :] = (s > t)
    lm[:, 3, :] = (s < t)
    c["lmats"] = (lm * (-1.0 / 16.0)).astype(np.float32)
    mk = np.zeros((128, 2, 4, 128), np.float32)
    mk[:, 0, :, :] = (s <= t)[:, None, :]
    mk[:, 1, :, :] = (s >= t)[:, None, :]
    c["masks"] = mk.astype(ml_dtypes.bfloat16)
    return c


def _rope_tables(half):
    pos = np.arange(half * T_CORE, (half + 1) * T_CORE)
    row = (pos // 64).astype(np.float32)
    col = (pos % 64).astype(np.float32)
    freqs = (np.float32(10000.0) ** (-np.arange(16, dtype=np.float32) / np.float32(16))).astype(np.float32)
    ang_r = (row[:, None] * freqs).astype(np.float32)
    ang_c = (col[:, None] * freqs).astype(np.float32)
    tab = np.zeros((T_CORE, 2, 64), np.float32)
    for base, ang in ((0, ang_r), (32, ang_c)):
        cs, sn = np.cos(ang).astype(np.float32), np.sin(ang).astype(np.float32)
        tab[:, 0, base:base + 16] = cs
        tab[:, 0, base + 16:base + 32] = cs
        tab[:, 1, base:base + 16] = -sn
        tab[:, 1, base + 16:base + 32] = sn
    return tab


_PROG_CACHE = {}


def kernel(x, c, ctx, c_ctx, _n_layers=4, _stage=99, **w):
    n = 8
    if (_n_layers, _stage) not in _PROG_CACHE:
        _PROG_CACHE[(_n_layers, _stage)] = Prog(_n_layers, stage=_stage).build()
    nc = _PROG_CACHE[(_n_layers, _stage)]
    consts = _consts()
    wf = {k: np.ascontiguousarray(np.asarray(w[k], dtype=np.float32)) for k in _W_NAMES}
    vecs = np.concatenate([wf["c_conv_w"][0], wf["c_conv_b"][0][None], wf["c_b_ra"][0], wf["c_b_ri"][0], wf["c_lam"][0]], 0)
    wf["c_vecs"] = np.ascontiguousarray(vecs.reshape(11, 10, 128).transpose(2, 0, 1))
    in_maps = []
    for core in range(n):
        b, half = core // 2, core % 2
        cvec = np.stack([np.asarray(c[b], np.float32), np.asarray(c_ctx, np.float32)], 0)
        cT = np.ascontiguousarray(cvec.reshape(2, 8, 128).transpose(2, 1, 0).reshape(128, 16))
        fl = np.zeros((128, 2), np.float32)
        fl[:, half] = 1.0
        m = dict(wf)
        m.update(consts)
        m["x"] = np.ascontiguousarray(np.asarray(x[b, half * T_CORE:(half + 1) * T_CORE], np.float32))
        m["ctx"] = np.ascontiguousarray(np.asarray(ctx[b], np.float32))
        m["cT"] = cT
        m["flags"] = fl
        m["rope"] = _rope_tables(half)
        in_maps.append(m)
    res = run_bass_kernel_spmd(nc, in_maps, core_ids=list(range(n)))
    out = np.zeros((4, 2 * T_CORE, 1024), np.float32)
    for core in range(n):
        b, half = core // 2, core % 2
        out[b, half * T_CORE:(half + 1) * T_CORE] = res.results[core]["out"]
    return out
```
